# Optimizing a Trainium2 kernel written in Bass

```python
import functools
import jax, jax.numpy as jnp
from jax import lax
import numpy as np

D_MODEL = 1024
BATCH = 4
SEQ = 8192
DEPTH = 4

MEM_LEN = 256
Q_BLOCK = 128
MAX_GROUPS = 8
N_BRANCHES = 4
BRANCH_WIDTH = 256
MLA_HEADS = 4
MLA_NOPE = 64
MLA_ROPE = 32
MLA_V = 64
MLA_Q_RANK = 384
MLA_KV_RANK = 256
ROPE_THETA = 10000.0
SB_HEADS = 4
SB_HEAD_DIM = 64
FOX_HEADS = 4
FOX_HEAD_DIM = 64
MEM_HEADS = 4
MEM_HEAD_DIM = 64
MERGE_RANK = 128
RMS_EPS = 1e-6
LN_EPS = 1e-5
DEEPNORM_ALPHA = (2 * DEPTH) ** 0.25
DEEPNORM_BETA = (8 * DEPTH) ** -0.25

IN_SPLITS = (
    MLA_Q_RANK,
    MLA_KV_RANK,
    MLA_ROPE,
    3 * SB_HEADS * SB_HEAD_DIM,
    3 * FOX_HEADS * FOX_HEAD_DIM,
    FOX_HEADS,
    MEM_HEADS * MEM_HEAD_DIM,
    N_BRANCHES * BRANCH_WIDTH,
    MERGE_RANK,
)
IN_WIDTH = sum(IN_SPLITS)

kernel_name = "hybrid_mla_stickbreak_fox_memory_deepnorm"


def _split_last(t, sizes):
    out, start = [], 0
    for n in sizes:
        out.append(t[..., start:start + n])
        start += n
    return out


def _rms_norm(x, g):
    xf = x.astype(jnp.float32)
    y = xf * lax.rsqrt(jnp.mean(xf * xf, axis=-1, keepdims=True) + RMS_EPS)
    return (y * g.astype(jnp.float32)).astype(x.dtype)


def _layer_norm(x, g, b):
    xf = x.astype(jnp.float32)
    mu = jnp.mean(xf, axis=-1, keepdims=True)
    var = jnp.mean(jnp.square(xf - mu), axis=-1, keepdims=True)
    y = (xf - mu) * lax.rsqrt(var + LN_EPS)
    return (y * g.astype(jnp.float32) + b.astype(jnp.float32)).astype(x.dtype)


def _rope(x, cos, sin):
    x1, x2 = jnp.split(x, 2, axis=-1)
    return jnp.concatenate([x1 * cos - x2 * sin, x1 * sin + x2 * cos], axis=-1)


def _causal_sweep(block_fn, s):
    n_blocks = s // Q_BLOCK
    n_groups = min(MAX_GROUPS, n_blocks)
    bounds = [(g * n_blocks) // n_groups for g in range(n_groups + 1)]
    outs = []
    for g in range(n_groups):
        b0, b1 = bounds[g], bounds[g + 1]
        o = lax.map(functools.partial(block_fn, kend=b1 * Q_BLOCK), jnp.arange(b0, b1))
        nb, b, qn, h, dv = o.shape
        outs.append(o.transpose(1, 0, 2, 3, 4).reshape(b, nb * qn, h, dv))
    return jnp.concatenate(outs, axis=1)


def _causal_softmax_attention(q, k, v, log_decay=None):
    b, s, h, d = q.shape
    q = q * (d ** -0.5)
    c_t = None if log_decay is None else log_decay.transpose(0, 2, 1)

    def block(i, kend):
        start = i * Q_BLOCK
        qb = lax.dynamic_slice_in_dim(q, start, Q_BLOCK, axis=1)
        kb, vb = k[:, :kend], v[:, :kend]
        logits = jnp.einsum('bqhd,bkhd->bhqk', qb, kb, preferred_element_type=jnp.float32)
        if c_t is not None:
            cq = lax.dynamic_slice_in_dim(c_t, start, Q_BLOCK, axis=2)
            logits = logits + (cq[..., :, None] - c_t[..., None, :kend])
        causal = jnp.arange(kend)[None, :] <= (start + jnp.arange(Q_BLOCK))[:, None]
        logits = jnp.where(causal, logits, -jnp.inf)
        m = jnp.max(logits, axis=-1, keepdims=True)
        p = jnp.exp(logits - m)
        denom = jnp.sum(p, axis=-1).transpose(0, 2, 1)[..., None]
        o = jnp.einsum('bhqk,bkhd->bqhd', p.astype(v.dtype), vb, preferred_element_type=jnp.float32)
        return (o / denom).astype(v.dtype)

    return _causal_sweep(block, s)


def _stick_breaking_attention(q, k, v):
    b, s, h, d = q.shape
    q = q * (d ** -0.5)
    tri = jnp.tril(jnp.ones((Q_BLOCK, Q_BLOCK), jnp.float32))

    def block(i, kend):
        start = i * Q_BLOCK
        nk = kend // Q_BLOCK
        qb = lax.dynamic_slice_in_dim(q, start, Q_BLOCK, axis=1)
        kb, vb = k[:, :kend], v[:, :kend]
        z = jnp.einsum('bqhd,bkhd->bhqk', qb, kb, preferred_element_type=jnp.float32)
        strict = jnp.arange(kend)[None, :] < (start + jnp.arange(Q_BLOCK))[:, None]
        log_keep = jnp.where(strict, jax.nn.log_sigmoid(-z), 0.0)
        local = jnp.einsum('bhqnc,cr->bhqnr', log_keep.reshape(b, h, Q_BLOCK, nk, Q_BLOCK), tri)
        tot = local[..., 0]
        cross = lax.cumsum(tot, axis=3, reverse=True) - tot
        incl = (local + cross[..., None]).reshape(b, h, Q_BLOCK, kend)
        a = jnp.where(strict, jnp.exp(z + incl), 0.0).astype(v.dtype)
        return jnp.einsum('bhqk,bkhd->bqhd', a, vb)

    return _causal_sweep(block, s)


def _memory_attention(q, k, v):
    logits = jnp.einsum('bshd,bmhd->bhsm', q * (q.shape[-1] ** -0.5), k,
                        preferred_element_type=jnp.float32)
    p = jax.nn.softmax(logits, axis=-1).astype(v.dtype)
    return jnp.einsum('bhsm,bmhd->bshd', p, v)


def _layer(x, mem, cos, sin, w_in, q_norm, w_qb, kv_norm, w_kvb, fox_bias,
           w_mem_kv, w_merge_up, w_branch, w_out, ln_g, ln_b):
    b, s, _ = x.shape
    h = x @ w_in
    (c_q, c_kv, k_rope, sb_qkv, fox_qkv, fox_f, mem_q, gate_z, merge_r) = _split_last(h, IN_SPLITS)

    q = (_rms_norm(c_q, q_norm) @ w_qb).reshape(b, s, MLA_HEADS, MLA_NOPE + MLA_ROPE)
    q_nope, q_pe = q[..., :MLA_NOPE], q[..., MLA_NOPE:]
    q_pe = _rope(q_pe, cos[:, :, None, :], sin[:, :, None, :])
    kv = (_rms_norm(c_kv, kv_norm) @ w_kvb).reshape(b, s, MLA_HEADS, MLA_NOPE + MLA_V)
    k_nope, v_mla = kv[..., :MLA_NOPE], kv[..., MLA_NOPE:]
    k_pe = _rope(k_rope[:, :, None, :], cos[:, :, None, :], sin[:, :, None, :])
    k_pe = jnp.broadcast_to(k_pe, (b, s, MLA_HEADS, MLA_ROPE))
    y_mla = _causal_softmax_attention(jnp.concatenate([q_nope, q_pe], -1),
                                      jnp.concatenate([k_nope, k_pe], -1), v_mla)

    sq, sk, sv = [t.reshape(b, s, SB_HEADS, SB_HEAD_DIM) for t in jnp.split(sb_qkv, 3, axis=-1)]
    y_sb = _stick_breaking_attention(sq, sk, sv)

    fq, fk, fv = [t.reshape(b, s, FOX_HEADS, FOX_HEAD_DIM) for t in jnp.split(fox_qkv, 3, axis=-1)]
    log_f = jax.nn.log_sigmoid(fox_f.astype(jnp.float32) + fox_bias.astype(jnp.float32))
    c = jnp.cumsum(log_f, axis=1)
    y_fox = _causal_softmax_attention(fq, fk, fv, log_decay=c)

    mk, mv = jnp.split(mem @ w_mem_kv, 2, axis=-1)
    mk = mk.reshape(b, MEM_LEN, MEM_HEADS, MEM_HEAD_DIM)
    mv = mv.reshape(b, MEM_LEN, MEM_HEADS, MEM_HEAD_DIM)
    y_mem = _memory_attention(mem_q.reshape(b, s, MEM_HEADS, MEM_HEAD_DIM), mk, mv)

    branches = (y_mla, y_sb, y_fox, y_mem)
    gate_z = gate_z.reshape(b, s, N_BRANCHES, BRANCH_WIDTH)
    merge_g = jax.nn.sigmoid(merge_r @ w_merge_up).reshape(b, s, N_BRANCHES, D_MODEL)
    merged = jnp.zeros_like(x)
    for n in range(N_BRANCHES):
        yb = branches[n].reshape(b, s, BRANCH_WIDTH) * jax.nn.silu(gate_z[:, :, n])
        merged = merged + merge_g[:, :, n] * (yb @ w_branch[n])
    out = merged @ w_out

    return _layer_norm(DEEPNORM_ALPHA * x + out, ln_g, ln_b)


def setup_inputs(seed: int = 0) -> dict:
    key = jax.random.key(seed)
    ks = jax.random.split(key, 16)
    f32 = jnp.float32

    def nrm(k, shape, scale):
        return jax.random.normal(k, shape, f32) * scale

    x = nrm(ks[0], (BATCH, SEQ, D_MODEL), 1.0)
    mem = nrm(ks[1], (BATCH, MEM_LEN, D_MODEL), 1.0)
    offsets = jax.random.randint(ks[2], (BATCH, 1), 0, 4096, dtype=jnp.int32)
    positions = (jnp.arange(SEQ, dtype=jnp.int32)[None, :] + offsets).astype(jnp.int32)

    w_in = nrm(ks[3], (DEPTH, D_MODEL, IN_WIDTH), D_MODEL ** -0.5)
    mla_q_norm = 1.0 + nrm(ks[4], (DEPTH, MLA_Q_RANK), 0.02)
    mla_w_qb = nrm(ks[5], (DEPTH, MLA_Q_RANK, MLA_HEADS * (MLA_NOPE + MLA_ROPE)), MLA_Q_RANK ** -0.5)
    mla_kv_norm = 1.0 + nrm(ks[6], (DEPTH, MLA_KV_RANK), 0.02)
    mla_w_kvb = nrm(ks[7], (DEPTH, MLA_KV_RANK, MLA_HEADS * (MLA_NOPE + MLA_V)), MLA_KV_RANK ** -0.5)
    fox_forget_bias = jax.random.uniform(ks[8], (DEPTH, FOX_HEADS), f32, 1.0, 4.0)
    w_mem_kv = nrm(ks[9], (DEPTH, D_MODEL, 2 * MEM_HEADS * MEM_HEAD_DIM), D_MODEL ** -0.5)
    w_merge_up = nrm(ks[10], (DEPTH, MERGE_RANK, N_BRANCHES * D_MODEL), MERGE_RANK ** -0.5)
    w_branch = nrm(ks[11], (DEPTH, N_BRANCHES, BRANCH_WIDTH, D_MODEL), BRANCH_WIDTH ** -0.5 * DEEPNORM_BETA)
    w_out = nrm(ks[12], (DEPTH, D_MODEL, D_MODEL), D_MODEL ** -0.5 * DEEPNORM_BETA)
    ln_gain = 1.0 + nrm(ks[13], (DEPTH, D_MODEL), 0.02)
    ln_bias = nrm(ks[14], (DEPTH, D_MODEL), 0.02)
    return {"x": x, "mem": mem, "positions": positions, "w_in": w_in,
            "mla_q_norm": mla_q_norm, "mla_w_qb": mla_w_qb,
            "mla_kv_norm": mla_kv_norm, "mla_w_kvb": mla_w_kvb,
            "fox_forget_bias": fox_forget_bias, "w_mem_kv": w_mem_kv,
            "w_merge_up": w_merge_up, "w_branch": w_branch, "w_out": w_out,
            "ln_gain": ln_gain, "ln_bias": ln_bias}


def reference(x, mem, positions, w_in, mla_q_norm, mla_w_qb, mla_kv_norm, mla_w_kvb,
              fox_forget_bias, w_mem_kv, w_merge_up, w_branch, w_out, ln_gain, ln_bias):
    inv_freq = ROPE_THETA ** (-jnp.arange(0, MLA_ROPE, 2, dtype=jnp.float32) / MLA_ROPE)
    ang = positions.astype(jnp.float32)[..., None] * inv_freq
    cos = jnp.cos(ang).astype(x.dtype)
    sin = jnp.sin(ang).astype(x.dtype)
    for l in range(DEPTH):
        x = _layer(x, mem, cos, sin, w_in[l], mla_q_norm[l], mla_w_qb[l],
                   mla_kv_norm[l], mla_w_kvb[l], fox_forget_bias[l], w_mem_kv[l],
                   w_merge_up[l], w_branch[l], w_out[l], ln_gain[l], ln_bias[l])
    return x
```

```python
import numpy as np
import concourse.bass as bass
import concourse.mybir as mybir
from concourse.bass_utils import run_bass_kernel_spmd

F32 = mybir.dt.float32
BF16 = mybir.dt.bfloat16
I32 = mybir.dt.int32
AF = mybir.ActivationFunctionType
ALU = mybir.AluOpType
AX = mybir.AxisListType


class Res:
    __slots__ = ("name", "w", "r")

    def __init__(self, name):
        self.name = name
        self.w = None
        self.r = {}


class Sched:
    def __init__(self, nc, n_dma_sems=40):
        self.nc = nc
        self.eng = {"pe": nc.tensor, "act": nc.scalar, "dve": nc.vector, "pool": nc.gpsimd, "sp": nc.sync}
        self.sem = {k: nc.alloc_semaphore("prog_" + k) for k in ("pe", "act", "dve", "pool")}
        self.cnt = {k: 0 for k in self.sem}
        self.dsem = [nc.alloc_semaphore("dma%d" % i) for i in range(n_dma_sems)]
        self.dval = [0] * n_dma_sems
        self.dnext = 0
        self.dnext_q = {"sp": 0, "pool": n_dma_sems - 8}
        self.drange = {"sp": (0, n_dma_sems - 8), "pool": (n_dma_sems - 8, n_dma_sems)}
        self.waited = {k: {} for k in self.eng}
        self.out_tickets = []

    def _wait(self, e, ticket):
        if ticket is None:
            return
        key, val = ticket
        if key == "pe" and e == "pe":
            return
        w = self.waited[e]
        if w.get(key, 0) >= val:
            return
        w[key] = val
        sem = self.sem[key] if isinstance(key, str) else self.dsem[key]
        self.eng[e].wait_ge(sem, val)

    def _deps(self, e, reads, writes):
        for r in reads:
            self._wait(e, r.w)
        for r in writes:
            self._wait(e, r.w)
            for k, v in r.r.items():
                self._wait(e, (k, v))

    def _commit(self, ticket, reads, writes):
        for r in reads:
            if r.r.get(ticket[0], 0) < ticket[1]:
                r.r[ticket[0]] = ticket[1]
        for r in writes:
            r.w = ticket
            r.r = {}

    def op(self, e, fn, reads=(), writes=()):
        self._deps(e, reads, writes)
        ins = fn(self.eng[e])
        self.cnt[e] += 1
        ins.then_inc(self.sem[e], 1)
        t = (e, self.cnt[e])
        self._commit(t, reads, writes)
        return t

    def dma(self, out, in_, reads=(), writes=(), q="sp"):
        i = self.dnext_q[q]
        lo, hi = self.drange[q]
        self.dnext_q[q] = lo + (i + 1 - lo) % (hi - lo)
        if self.dval[i]:
            self._wait(q, (i, self.dval[i]))
        self._deps(q, reads, writes)
        self.dval[i] += 16
        self.eng[q].dma_start(out=out, in_=in_).then_inc(self.dsem[i], 16)
        t = (i, self.dval[i])
        self._commit(t, reads, writes)
        return t

    def barrier(self):
        for e in ("pe", "act", "dve", "pool", "sp"):
            for k in self.sem:
                if self.cnt[k]:
                    self._wait(e, (k, self.cnt[k]))
            for i, v in enumerate(self.dval):
                if v:
                    self._wait(e, (i, v))

    def finish(self, resources):
        for r in resources:
            self._wait("sp", r.w)


class T:
    def __init__(self, ap, name, psum=False, res=None):
        self.ap = ap
        self.res = res if res is not None else Res(name)
        self.psum = psum

    def __getitem__(self, idx):
        return self.ap[idx]


D_MODEL = 1024
IN_W = 3620
BLK2 = ((0, 3, 4, 7), (1, 2, 5, 6))
NEG = -30000.0
RMS_EPS = 1e-6
LN_EPS = 1e-5
WSEG = {"cq": (0, 384), "ckv": (384, 256), "kr": (640, 32), "sbq": (672, 256), "sbk": (928, 256), "sbv": (1184, 256),
        "fq": (1440, 256), "fk": (1696, 256), "fv": (1952, 256), "ff": (2208, 4), "mq": (2212, 256),
        "gz": (2468, 1024), "mr": (3492, 128)}
WORDER = ["cq", "ckv", "kr", "sbq", "fq", "mq", "sbk", "fk", "mr", "ff", "sbv", "fv", "gz"]
WOFF = {}
_o = 0
for _k in WORDER:
    WOFF[_k] = _o
    _o += WSEG[_k][1]
assert _o == IN_W


def kv_place(n, NP):
    if NP == 1:
        return 0, n
    g, m = divmod(n, 8)
    r = 0 if m in BLK2[0] else 1
    return r, g * 4 + BLK2[r].index(m)


def build_program(S, L, NP, depth_alpha):
    nc = bass.Bass("TRN2", target_bir_lowering=False)
    D = D_MODEL
    NOWN = S // NP
    NBO = NOWN // 128
    NBLK = S // 128
    NQG = NOWN // 512
    NSB = S // 1024
    NM = 4 if NP == 1 else 8
    dt = nc.dram_tensor

    def din(name, shape, dtype=F32):
        return dt(name, list(shape), dtype, kind="ExternalInput").ap()

    x_in = din("x", [NOWN, D])
    mem_in = din("mem", [256, D])
    pos_in = din("pos", [128, NBO], I32)
    masks_in = din("masks", [128, 2 * NM, 512])
    w_in_d = din("w_in", [L, D, IN_W])
    qn_d = din("mla_q_norm", [L, 384])
    wqb_d = din("mla_w_qb", [L, 384, 384])
    kvn_d = din("mla_kv_norm", [L, 256])
    wkvb_d = din("mla_w_kvb", [L, 256, 512])
    fb_d = din("fox_forget_bias", [L, 4])
    wmkv_d = din("w_mem_kv", [L, D, 512])
    wmu_d = din("w_merge_up", [L, 128, 4096])
    wbr_d = din("w_branch", [L, 4, 256, D])
    wout_d = din("w_out", [L, D, D])
    lng_d = din("ln_gain", [L, D])
    lnb_d = din("ln_bias", [L, D])
    y_out = dt("y", [NOWN, D], F32, kind="ExternalOutput").ap()

    XR = dt("XR", [NOWN, D], F32).ap()
    KT_in = dt("KT_in", [800, NOWN], BF16).ap()
    V_in = dt("V_in", [NOWN, 768], BF16).ap()
    LF_in = dt("LF_in", [4, NOWN], F32).ap()
    if NP == 1:
        KT_g, V_g, LF_g = KT_in, V_in, LF_in
    else:
        KT_g = dt("KT_g", [NP * 800, NOWN], BF16).ap()
        V_g = dt("V_g", [NP * NOWN, 768], BF16).ap()
        LF_g = dt("LF_g", [NP * 4, NOWN], F32).ap()
    QT = dt("QT", [1152, NOWN], BF16).ap()
    GZ = dt("GZ", [NOWN, D], F32).ap()
    MR = dt("MR", [128, NOWN], BF16).ap()
    Y = dt("Y", [NOWN, D], F32).ap()
    CAK = dt("CAK", [4, 6, S], BF16).ap()
    CAQ = dt("CAQ", [4, 6, NOWN], BF16).ap()

    S_ = Sched(nc)

    import os
    KMAX = int(os.environ.get("KMAX", "1000000000"))
    nops = [0]

    def OP(e, fn, r=(), w=()):
        nops[0] += 1
        if nops[0] > KMAX:
            return None
        return S_.op(e, fn, [t.res for t in r if not t.psum], [t.res for t in w] + [t.res for t in r if t.psum])

    def DMA(out, in_, r=(), w=(), q="sp"):
        nops[0] += 1
        if nops[0] > KMAX:
            return None
        if nops[0] == KMAX:
            print("LAST DMA", q, out.shape, in_.shape)
        return S_.dma(out, in_, [t.res for t in r], [t.res for t in w], q)

    from contextlib import ExitStack

    uid = [0]

    def sb(es, name, shape, dtype):
        uid[0] += 1
        nm = "s%d_%s" % (uid[0], name)
        return T(es.enter_context(nc.sbuf_tensor(nm, list(shape), dtype)).ap(), nm)

    PB = [T(nc.alloc_psum_tensor("pb%d" % i, [128, 512], F32).ap(), "pb%d" % i, psum=True) for i in range(6)]
    PT = [T(nc.alloc_psum_tensor("pt%d" % i, [128, 1024], BF16).ap(), "pt%d" % i, psum=True) for i in range(2)]
    rr = {"pb": 0, "pt": 0}

    def pbank():
        rr["pb"] = (rr["pb"] + 1) % 6
        return PB[rr["pb"]]

    def ptbank():
        rr["pt"] = (rr["pt"] + 1) % 2
        return PT[rr["pt"]]

    ges = ExitStack()
    ident = sb(ges, "ident", [128, 128], BF16)
    nident = sb(ges, "nident", [128, 128], BF16)
    tri = sb(ges, "tri", [128, 128], BF16)
    onec = sb(ges, "onec", [128, 1], BF16)
    masks = sb(ges, "masks", [128, 2 * NM, 512], BF16)
    rope = sb(ges, "rope", [128, NBO, 48], F32)
    ropeq = sb(ges, "ropeq", [128, NBO, 48], F32)
    memT = sb(ges, "memT", [128, 8, 256], BF16)
    mkT = sb(ges, "mkT", [64, 4, 256], BF16)
    mv = sb(ges, "mv", [128, 2, 4, 65], BF16)

    OP("pool", lambda e: e.memset(ident.ap, 0.0), w=[ident])
    OP("pool", lambda e: e.affine_select(out=ident.ap, in_=ident.ap, pattern=[[-1, 128]], compare_op=ALU.not_equal,
                                         fill=1.0, base=0, channel_multiplier=1), r=[ident], w=[ident])
    OP("pool", lambda e: e.memset(nident.ap, 0.0), w=[nident])
    OP("pool", lambda e: e.affine_select(out=nident.ap, in_=nident.ap, pattern=[[-1, 128]], compare_op=ALU.not_equal,
                                         fill=-1.0, base=0, channel_multiplier=1), r=[nident], w=[nident])
    OP("pool", lambda e: e.memset(tri.ap, 1.0), w=[tri])
    OP("pool", lambda e: e.affine_select(out=tri.ap, in_=tri.ap, pattern=[[-1, 128]], compare_op=ALU.is_ge,
                                         fill=0.0, base=0, channel_multiplier=1), r=[tri], w=[tri])
    OP("pool", lambda e: e.memset(onec.ap, 1.0), w=[onec])
    OP("pool", lambda e: e.memset(mv.ap, 1.0), w=[mv])

    with ExitStack() as es:
        mstage = sb(es, "mstage", [128, 2 * NM, 512], F32)
        DMA(mstage.ap, masks_in, w=[mstage])
        OP("dve", lambda e: e.tensor_copy(out=masks.ap, in_=mstage.ap), r=[mstage], w=[masks])
        posi = sb(es, "posi", [128, NBO], I32)
        posf = sb(es, "posf", [128, NBO], F32)
        invf = sb(es, "invf", [128, 16], F32)
        ang = sb(es, "ang", [128, NBO, 16], F32)
        tq = sb(es, "tq", [128, NBO, 16], F32)
        ti = sb(es, "ti", [128, NBO, 16], I32)
        rr_ = sb(es, "rr_", [128, NBO, 16], F32)
        rc_ = sb(es, "rc_", [128, NBO, 16], F32)
        mk_ = sb(es, "mk_", [128, NBO, 16], F32)
        DMA(posi.ap, pos_in, w=[posi])
        OP("dve", lambda e: e.tensor_copy(out=posf.ap, in_=posi.ap), r=[posi], w=[posf])
        for i in range(16):
            v = float(np.float32(10000.0) ** np.float32(-(2.0 * i) / 32.0))
            OP("pool", lambda e, i=i, v=v: e.memset(invf[:, i:i + 1], v), w=[invf])
        OP("dve", lambda e: e.tensor_tensor(out=ang.ap, in0=posf.ap.unsqueeze(2).to_broadcast([128, NBO, 16]),
                                            in1=invf.ap.unsqueeze(1).to_broadcast([128, NBO, 16]), op=ALU.mult),
           r=[posf, invf], w=[ang])
        TWO_PI = 2.0 * np.pi
        C1 = 6.28125
        C2 = TWO_PI - C1
        OP("dve", lambda e: e.tensor_scalar(out=tq.ap, in0=ang.ap, scalar1=float(1.0 / TWO_PI), scalar2=None, op0=ALU.mult), r=[ang], w=[tq])
        OP("dve", lambda e: e.tensor_copy(out=ti.ap, in_=tq.ap), r=[tq], w=[ti])
        OP("dve", lambda e: e.tensor_copy(out=tq.ap, in_=ti.ap), r=[ti], w=[tq])
        OP("dve", lambda e: e.scalar_tensor_tensor(out=rr_.ap, in0=tq.ap, scalar=-C1, in1=ang.ap, op0=ALU.mult, op1=ALU.add), r=[tq, ang], w=[rr_])
        OP("dve", lambda e: e.scalar_tensor_tensor(out=rr_.ap, in0=tq.ap, scalar=-C2, in1=rr_.ap, op0=ALU.mult, op1=ALU.add), r=[tq, rr_], w=[rr_])

        def wrap(t):
            OP("dve", lambda e: e.tensor_scalar(out=mk_.ap, in0=t.ap, scalar1=float(np.pi), scalar2=float(-TWO_PI), op0=ALU.is_gt, op1=ALU.mult), r=[t], w=[mk_])
            OP("dve", lambda e: e.tensor_tensor(out=t.ap, in0=t.ap, in1=mk_.ap, op=ALU.add), r=[t, mk_], w=[t])
            OP("dve", lambda e: e.tensor_scalar(out=mk_.ap, in0=t.ap, scalar1=float(-np.pi), scalar2=float(TWO_PI), op0=ALU.is_lt, op1=ALU.mult), r=[t], w=[mk_])
            OP("dve", lambda e: e.tensor_tensor(out=t.ap, in0=t.ap, in1=mk_.ap, op=ALU.add), r=[t, mk_], w=[t])
            OP("dve", lambda e: e.tensor_scalar(out=t.ap, in0=t.ap, scalar1=3.1415925, scalar2=-3.1415925, op0=ALU.min, op1=ALU.max), r=[t], w=[t])

        wrap(rr_)
        OP("dve", lambda e: e.tensor_scalar(out=rc_.ap, in0=rr_.ap, scalar1=float(np.pi / 2), scalar2=None, op0=ALU.add), r=[rr_], w=[rc_])
        wrap(rc_)
        OP("act", lambda e: e.activation(out=rope[:, :, 0:16], in_=rc_.ap, func=AF.Sin), r=[rc_], w=[rope])
        OP("act", lambda e: e.activation(out=rope[:, :, 16:32], in_=rr_.ap, func=AF.Sin), r=[rr_], w=[rope])
        OP("act", lambda e: e.activation(out=rope[:, :, 32:48], in_=rc_.ap, func=AF.Sin), r=[rc_], w=[rope])
        OP("dve", lambda e: e.tensor_scalar(out=ropeq.ap, in0=rope.ap, scalar1=float(96.0 ** -0.5), scalar2=None, op0=ALU.mult), r=[rope], w=[ropeq])
        mst = sb(es, "mst", [128, 2, D], F32)
        mbf = sb(es, "mbf", [128, 2, D], BF16)
        DMA(mst.ap, mem_in.rearrange("(c p) d -> p c d", p=128), w=[mst])
        OP("dve", lambda e: e.tensor_copy(out=mbf.ap, in_=mst.ap), r=[mst], w=[mbf])
        for mc in range(2):
            pt = ptbank()

            def trm(e, mc=mc, pt=pt):
                for c in range(8):
                    i = e.transpose(out=pt[:, c * 128:(c + 1) * 128], in_=mbf[:, mc, c * 128:(c + 1) * 128], identity=ident.ap)
                return i
            OP("pe", trm, r=[mbf, ident], w=[pt])
            OP("act", lambda e, mc=mc, pt=pt: e.copy(out=memT[:, :, mc * 128:(mc + 1) * 128],
                                                     in_=pt.ap.rearrange("p (c t) -> p c t", t=128)), r=[pt], w=[memT])
        S_.barrier()
        print("nops after setup", nops[0])

    F0 = WOFF["sbq"]

    def phase_a(l):
        with ExitStack() as es:
            win = sb(es, "win", [128, 8, IN_W], BF16)
            for k in WORDER:
                c0, wd = WSEG[k]
                DMA(win[:, :, WOFF[k]:WOFF[k] + wd], w_in_d[l][:, c0:c0 + wd].rearrange("(c p) n -> p c n", p=128), w=[win], q="pool")
            wqb = sb(es, "wqb", [128, 3, 384], BF16)
            DMA(wqb.ap, wqb_d[l].rearrange("(c p) n -> p c n", p=128), w=[wqb], q="pool")
            wkvb = sb(es, "wkvb", [128, 2, 512], BF16)
            DMA(wkvb.ap, wkvb_d[l].rearrange("(c p) n -> p c n", p=128), w=[wkvb], q="pool")
            wmkv = sb(es, "wmkv", [128, 8, 512], BF16)
            DMA(wmkv.ap, wmkv_d[l].rearrange("(c p) n -> p c n", p=128), w=[wmkv], q="pool")
            gq = sb(es, "gq", [128, 384], F32)
            DMA(gq.ap, qn_d[l].partition_broadcast(128), w=[gq])
            gkv = sb(es, "gkv", [128, 256], F32)
            DMA(gkv.ap, kvn_d[l].partition_broadcast(128), w=[gkv])
            fbn = sb(es, "fbn", [4, 1], F32)
            DMA(fbn.ap, fb_d[l:l + 1, :].rearrange("o h -> h o"), w=[fbn])
            OP("dve", lambda e: e.tensor_scalar(out=fbn.ap, in0=fbn.ap, scalar1=-1.0, scalar2=None, op0=ALU.mult), r=[fbn], w=[fbn])
            for h in range(4):
                pb = pbank()

                def mmk(e, h=h, pb=pb):
                    for c in range(8):
                        i = e.matmul(pb[0:64, 0:256], lhsT=wmkv[:, c, h * 64:(h + 1) * 64], rhs=memT[:, c, :], start=(c == 0), stop=(c == 7))
                    return i
                OP("pe", mmk, r=[wmkv, memT], w=[pb])
                OP("act", lambda e, h=h, pb=pb: e.copy(out=mkT[:, h, :], in_=pb[0:64, 0:256]), r=[pb], w=[mkT])
            for mc in range(2):
                pb = pbank()

                def mmv(e, mc=mc, pb=pb):
                    for c in range(8):
                        i = e.matmul(pb[:, 0:256], lhsT=memT[:, c, mc * 128:(mc + 1) * 128], rhs=wmkv[:, c, 256:512], start=(c == 0), stop=(c == 7))
                    return i
                OP("pe", mmv, r=[wmkv, memT], w=[pb])
                OP("act", lambda e, mc=mc, pb=pb: e.copy(out=mv[:, mc, :, 0:64], in_=pb[:, 0:256].rearrange("p (h d) -> p h d", d=64)), r=[pb], w=[mv])

            def dbl(name, shape, dtype):
                return [sb(es, name + str(i), shape, dtype) for i in range(3)]
            XT_ = [sb(es, "xt%d" % i, [128, D], F32) for i in range(4)]
            XB_ = dbl("xb", [128, D], BF16)
            XTT_ = dbl("xT", [128, 8, 128], BF16)
            JK_ = dbl("junk", [128, 384], F32)
            SS_ = dbl("ss", [128, 2], F32)
            RS_ = dbl("rs", [128, 2], F32)
            CN_ = dbl("cn", [128, 768], BF16)
            T4_ = dbl("t4", [128, 64], F32)
            CT_ = dbl("cT", [128, 6, 128], BF16)
            T8_ = dbl("t8", [128, 256], F32)
            QF_ = dbl("qf", [128, 384], BF16)
            QT_ = dbl("qT", [128, 4, 128], BF16)
            KN_ = dbl("kn", [128, 256], BF16)
            KNT_ = dbl("knT", [128, 2, 128], BF16)
            VT_ = dbl("vt", [128, 768], BF16)
            FT_ = dbl("FT", [128, 11, 128], BF16)
            LFA_ = dbl("lfa", [4, 128], F32)
            LFO_ = dbl("lfo", [4, 128], F32)
            GZT_ = dbl("gzt", [128, D], F32)
            src = x_in if l == 0 else XR
            def gen_a(tb):
                k2 = tb % 3
                xt, xb, xT, junk, ss, rs, cn, t4, cT, t8, qf, qT, kn, knT, vt, FT, lfa, lfo, gzt = (
                    XT_[tb % 4], XB_[k2], XTT_[k2], JK_[k2], SS_[k2], RS_[k2], CN_[k2], T4_[k2], CT_[k2], T8_[k2], QF_[k2],
                    QT_[k2], KN_[k2], KNT_[k2], VT_[k2], FT_[k2], LFA_[k2], LFO_[k2], GZT_[k2])
                rows = slice(tb * 128, (tb + 1) * 128)
                DMA(xt.ap, src[rows, :], w=[xt])
                yield
                OP("pool", lambda e: e.tensor_copy(out=xb.ap, in_=xt.ap), r=[xt], w=[xb])
                pt = ptbank()

                def trx(e, pt=pt, xb=xb):
                    for c in range(8):
                        i = e.transpose(out=pt[:, c * 128:(c + 1) * 128], in_=xb[:, c * 128:(c + 1) * 128], identity=ident.ap)
                    return i
                OP("pe", trx, r=[xb, ident], w=[pt])
                OP("act", lambda e, pt=pt, xT=xT: e.copy(out=xT.ap.rearrange("p c t -> p (c t)"), in_=pt.ap), r=[pt], w=[xT])

                def proj_tok(pb, n, woff, xT=xT):
                    def f(e):
                        for c in range(8):
                            i = e.matmul(pb[:, 0:n], lhsT=xT[:, c, :], rhs=win[:, c, woff:woff + n], start=(c == 0), stop=(c == 7))
                        return i
                    return f
                pa = pbank()
                OP("pe", proj_tok(pa, 384, WOFF["cq"]), r=[xT, win], w=[pa])
                pk = pbank()
                OP("pe", proj_tok(pk, 288, WOFF["ckv"]), r=[xT, win], w=[pk])
                OP("act", lambda e, pa=pa, junk=junk, ss=ss: e.activation(out=junk[:, 0:384], in_=pa[:, 0:384], func=AF.Square, accum_out=ss[:, 0:1]), r=[pa], w=[junk, ss])
                OP("act", lambda e, pk=pk, junk=junk, ss=ss: e.activation(out=junk[:, 0:256], in_=pk[:, 0:256], func=AF.Square, accum_out=ss[:, 1:2]), r=[pk], w=[junk, ss])
                OP("dve", lambda e, ss=ss, rs=rs: e.tensor_scalar(out=rs[:, 0:1], in0=ss[:, 0:1], scalar1=1.0 / 384, scalar2=RMS_EPS, op0=ALU.mult, op1=ALU.add), r=[ss], w=[rs])
                OP("dve", lambda e, ss=ss, rs=rs: e.tensor_scalar(out=rs[:, 1:2], in0=ss[:, 1:2], scalar1=1.0 / 256, scalar2=RMS_EPS, op0=ALU.mult, op1=ALU.add), r=[ss], w=[rs])
                OP("act", lambda e, rs=rs: e.activation(out=rs.ap, in_=rs.ap, func=AF.Sqrt), r=[rs], w=[rs])
                OP("dve", lambda e, rs=rs: e.reciprocal(out=rs.ap, in_=rs.ap), r=[rs], w=[rs])
                OP("dve", lambda e, pa=pa, rs=rs, cn=cn: e.scalar_tensor_tensor(out=cn[:, 0:384], in0=pa[:, 0:384], scalar=rs[:, 0:1], in1=gq.ap, op0=ALU.mult, op1=ALU.mult), r=[pa, rs, gq], w=[cn])
                OP("dve", lambda e, pk=pk, rs=rs, cn=cn: e.scalar_tensor_tensor(out=cn[:, 384:640], in0=pk[:, 0:256], scalar=rs[:, 1:2], in1=gkv.ap, op0=ALU.mult, op1=ALU.mult), r=[pk, rs, gkv], w=[cn])
                kr2 = pk[:, 256:288].rearrange("p (a i) -> p a i", a=2)
                OP("dve", lambda e, t4=t4, kr2=kr2: e.tensor_tensor(out=t4[:, 0:32].rearrange("p (a i) -> p a i", a=2), in0=kr2, in1=rope[:, tb, 0:32].rearrange("p (a i) -> p a i", a=2), op=ALU.mult), r=[pk, rope], w=[t4])
                OP("dve", lambda e, t4=t4, kr2=kr2: e.tensor_tensor(out=t4[:, 32:64].rearrange("p (a i) -> p a i", a=2), in0=kr2, in1=rope[:, tb, 16:48].rearrange("p (a i) -> p a i", a=2), op=ALU.mult), r=[pk, rope], w=[t4])
                OP("dve", lambda e, t4=t4, cn=cn: e.tensor_tensor(out=cn[:, 640:656], in0=t4[:, 0:16], in1=t4[:, 16:32], op=ALU.subtract), r=[t4], w=[cn])
                OP("dve", lambda e, t4=t4, cn=cn: e.tensor_tensor(out=cn[:, 656:672], in0=t4[:, 32:48], in1=t4[:, 48:64], op=ALU.add), r=[t4], w=[cn])
                banks = [pbank(), pbank(), pbank()]
                for bi in range(3):
                    def mmf(e, bi=bi, pb=banks[bi], xT=xT):
                        for j in range(4 * bi, min(4 * bi + 4, 11)):
                            for c in range(8):
                                i = e.matmul(pb[:, (j % 4) * 128:(j % 4 + 1) * 128], lhsT=win[:, c, F0 + j * 128:F0 + (j + 1) * 128], rhs=xT[:, c, :], start=(c == 0), stop=(c == 7))
                        if bi == 2:
                            for c in range(8):
                                i = e.matmul(pb[0:4, 384:512], lhsT=win[:, c, WOFF["ff"]:WOFF["ff"] + 4], rhs=xT[:, c, :], start=(c == 0), stop=(c == 7))
                        return i
                    OP("pe", mmf, r=[xT, win], w=[banks[bi]])
                FTf = FT.ap.rearrange("p c t -> p (c t)")
                OP("act", lambda e, FTf=FTf, pb=banks[0]: e.mul(out=FTf[:, 0:512], in_=pb.ap, mul=0.125), r=[banks[0]], w=[FT])
                OP("dve", lambda e, FTf=FTf, pb=banks[1]: e.tensor_scalar(out=FTf[:, 512:768], in0=pb[:, 0:256], scalar1=0.125, scalar2=None, op0=ALU.mult), r=[banks[1]], w=[FT])
                OP("dve", lambda e, FTf=FTf, pb=banks[1]: e.tensor_copy(out=FTf[:, 768:1024], in_=pb[:, 256:512]), r=[banks[1]], w=[FT])
                OP("act", lambda e, FTf=FTf, pb=banks[2]: e.copy(out=FTf[:, 1024:1408], in_=pb[:, 0:384]), r=[banks[2]], w=[FT])
                OP("act", lambda e, lfa=lfa, pb=banks[2]: e.activation(out=lfa.ap, in_=pb[0:4, 384:512], func=AF.Exp, scale=-1.0, bias=fbn.ap), r=[banks[2], fbn], w=[lfa])
                OP("act", lambda e, lfa=lfa: e.activation(out=lfa.ap, in_=lfa.ap, func=AF.Ln, bias=1.0), r=[lfa], w=[lfa])
                OP("dve", lambda e, lfa=lfa, lfo=lfo: e.tensor_scalar(out=lfo.ap, in0=lfa.ap, scalar1=-1.0, scalar2=None, op0=ALU.mult), r=[lfa], w=[lfo])
                DMA(QT[384:1152, rows].rearrange("(c p) t -> p c t", p=128), FT[:, 0:6, :], r=[FT])
                DMA(KT_in[288:800, rows].rearrange("(c p) t -> p c t", p=128), FT[:, 6:10, :], r=[FT])
                DMA(MR[:, rows], FT[:, 10, :], r=[FT])
                DMA(LF_in[:, rows], lfo.ap, r=[lfo])
                pv = pbank()
                OP("pe", proj_tok(pv, 512, WOFF["sbv"]), r=[xT, win], w=[pv])
                OP("dve", lambda e, vt=vt, pv=pv: e.tensor_copy(out=vt[:, 256:768], in_=pv.ap), r=[pv], w=[vt])
                for hf in range(2):
                    pg = pbank()
                    OP("pe", proj_tok(pg, 512, WOFF["gz"] + hf * 512), r=[xT, win], w=[pg])
                    OP("act", lambda e, gzt=gzt, pg=pg, hf=hf: e.activation(out=gzt[:, hf * 512:(hf + 1) * 512], in_=pg.ap, func=AF.Silu), r=[pg], w=[gzt])
                DMA(GZ[rows, :], gzt.ap, r=[gzt])

                yield
                pt = ptbank()

                def trc(e, pt=pt, cn=cn):
                    for c in range(5):
                        i = e.transpose(out=pt[:, c * 128:(c + 1) * 128], in_=cn[:, c * 128:(c + 1) * 128], identity=ident.ap)
                    i = e.transpose(out=pt[0:32, 640:768], in_=cn[:, 640:672], identity=ident.ap)
                    return i
                OP("pe", trc, r=[cn, ident], w=[pt])
                OP("act", lambda e, pt=pt, cT=cT: e.copy(out=cT.ap.rearrange("p c t -> p (c t)")[:, 0:640], in_=pt[:, 0:640]), r=[pt], w=[cT])
                OP("act", lambda e, pt=pt, cT=cT: e.copy(out=cT[0:32, 5, :], in_=pt[0:32, 640:768]), r=[pt], w=[cT])
                DMA(KT_in[256:288, rows], cT[0:32, 5, :], r=[cT])
                pq = pbank()

                def mmq(e, pq=pq, cT=cT):
                    for c in range(3):
                        i = e.matmul(pq[:, 0:384], lhsT=cT[:, c, :], rhs=wqb[:, c, :], start=(c == 0), stop=(c == 2))
                    return i
                OP("pe", mmq, r=[cT, wqb], w=[pq])
                pq3 = pq[:, 0:384].rearrange("p (h r) -> p h r", r=96)
                qf3 = qf.ap.rearrange("p (h r) -> p h r", r=96)
                OP("act", lambda e, pq3=pq3, qf3=qf3: e.mul(out=qf3[:, :, 0:64], in_=pq3[:, :, 0:64], mul=float(96.0 ** -0.5)), r=[pq], w=[qf])
                for h in range(4):
                    srcp = pq[:, h * 96 + 64:h * 96 + 96].rearrange("p (a i) -> p a i", a=2)
                    o = h * 64
                    OP("dve", lambda e, srcp=srcp, o=o: e.tensor_tensor(out=t8[:, o:o + 32].rearrange("p (a i) -> p a i", a=2), in0=srcp, in1=ropeq[:, tb, 0:32].rearrange("p (a i) -> p a i", a=2), op=ALU.mult), r=[pq, ropeq], w=[t8])
                    OP("dve", lambda e, srcp=srcp, o=o: e.tensor_tensor(out=t8[:, o + 32:o + 64].rearrange("p (a i) -> p a i", a=2), in0=srcp, in1=ropeq[:, tb, 16:48].rearrange("p (a i) -> p a i", a=2), op=ALU.mult), r=[pq, ropeq], w=[t8])
                    OP("dve", lambda e, o=o, h=h: e.tensor_tensor(out=qf[:, h * 96 + 64:h * 96 + 80], in0=t8[:, o:o + 16], in1=t8[:, o + 16:o + 32], op=ALU.subtract), r=[t8], w=[qf])
                    OP("dve", lambda e, o=o, h=h: e.tensor_tensor(out=qf[:, h * 96 + 80:h * 96 + 96], in0=t8[:, o + 32:o + 48], in1=t8[:, o + 48:o + 64], op=ALU.add), r=[t8], w=[qf])
                pkv = pbank()

                def mmkv(e, pkv=pkv, cT=cT):
                    for c in range(2):
                        i = e.matmul(pkv[:, 0:512], lhsT=cT[:, 3 + c, :], rhs=wkvb[:, c, :], start=(c == 0), stop=(c == 1))
                    return i
                OP("pe", mmkv, r=[cT, wkvb], w=[pkv])
                pkv3 = pkv.ap.rearrange("p (h r) -> p h r", r=128)
                OP("act", lambda e, pkv3=pkv3, vt=vt: e.copy(out=vt[:, 0:256].rearrange("p (h d) -> p h d", d=64), in_=pkv3[:, :, 64:128]), r=[pkv], w=[vt])
                OP("dve", lambda e, pkv3=pkv3, kn=kn: e.tensor_copy(out=kn.ap.rearrange("p (h d) -> p h d", d=64), in_=pkv3[:, :, 0:64]), r=[pkv], w=[kn])
                DMA(V_in[rows, :], vt.ap, r=[vt])
                yield
                pt = ptbank()

                def trq(e, pt=pt, qf=qf):
                    for h in range(4):
                        i = e.transpose(out=pt[0:96, h * 128:(h + 1) * 128], in_=qf[:, h * 96:(h + 1) * 96], identity=ident.ap)
                    return i
                OP("pe", trq, r=[qf, ident], w=[pt])
                OP("act", lambda e, pt=pt, qT=qT: e.copy(out=qT[0:96].rearrange("p h t -> p (h t)"), in_=pt[0:96, 0:512]), r=[pt], w=[qT])
                DMA(QT[0:384, rows].rearrange("(h r) t -> r h t", r=96), qT[0:96], r=[qT])
                pt = ptbank()

                def trk(e, pt=pt, kn=kn):
                    for c in range(2):
                        i = e.transpose(out=pt[:, c * 128:(c + 1) * 128], in_=kn[:, c * 128:(c + 1) * 128], identity=ident.ap)
                    return i
                OP("pe", trk, r=[kn, ident], w=[pt])
                OP("act", lambda e, pt=pt, knT=knT: e.copy(out=knT.ap.rearrange("p c t -> p (c t)"), in_=pt[:, 0:256]), r=[pt], w=[knT])
                DMA(KT_in[0:256, rows].rearrange("(c p) t -> p c t", p=128), knT.ap, r=[knT])

            gens = [gen_a(tb) for tb in range(NBO)]
            for tick in range(NBO + 3):
                for sk in range(4):
                    i = tick - sk
                    if 0 <= i < NBO:
                        next(gens[i], None)
            S_.barrier()

    KT_CH = [(0, 128), (128, 128), (256, 32), (288, 128), (416, 128), (544, 128), (672, 128)]
    VCH = min(1024, NOWN)

    def kt_grow(rk, R):
        for r0, n in KT_CH:
            if r0 <= R < r0 + n:
                return 2 * r0 + rk * n + (R - r0)
        raise AssertionError

    def load_cols_global(dst, src_g, row0, nrows, rows_per_rank, w):
        if NP == 1:
            DMA(dst, src_g[row0:row0 + nrows, :], w=w)
            return
        d4 = dst.rearrange("p (g m i) -> p g m i", m=8, i=128)
        for rk in range(2):
            gr = kt_grow(rk, row0) if rows_per_rank == 800 else rk * rows_per_rank + row0
            s5 = src_g[gr:gr + nrows, :].rearrange("p (g a b i) -> p g a b i", a=2, b=2, i=128)
            for a in range(2):
                for b in range(2):
                    DMA(d4[:, :, BLK2[rk][2 * a + b], :], s5[:, :, a, b, :], w=w)

    def load_v_global(dst, c0, w):
        if NP == 1:
            DMA(dst, V_g[:, c0:c0 + 64].rearrange("(n p) d -> p n d", p=128), w=w)
            return
        d4 = dst.rearrange("p (g m) d -> p g m d", m=8)
        GPC = VCH // 512
        for rk in range(2):
            for c in range(NOWN // VCH):
                base = c * 2 * VCH + rk * VCH
                s5 = V_g[base:base + VCH, c0:c0 + 64].rearrange("(g a b p) d -> p g a b d", a=2, b=2, p=128)
                for a in range(2):
                    for b in range(2):
                        DMA(d4[:, c * GPC:(c + 1) * GPC, BLK2[rk][2 * a + b], :], s5[:, :, a, b, :], w=w)

    def phase_fc(l):
        with ExitStack() as es:
            lfall = sb(es, "lfall", [4, S], F32)
            cz = sb(es, "cz", [4, S + 1], F32)
            CH = min(2048, S)
            onesf = sb(es, "onesf", [4, CH], F32)
            load_cols_global(lfall.ap, LF_g, 0, 4, 4, w=[lfall])
            OP("pool", lambda e: e.memset(onesf.ap, 1.0), w=[onesf])
            OP("pool", lambda e: e.memset(cz[:, 0:1], 0.0), w=[cz])
            for ch in range(S // CH):
                init = 0.0 if ch == 0 else cz[:, ch * CH:ch * CH + 1]
                OP("dve", lambda e, ch=ch, init=init: e.tensor_tensor_scan(out=cz[:, 1 + ch * CH:1 + (ch + 1) * CH], data0=onesf.ap, data1=lfall[:, ch * CH:(ch + 1) * CH], initial=init, op0=ALU.mult, op1=ALU.add), r=[onesf, lfall, cz], w=[cz])
            CW = 1024
            STK = [sb(es, "stk%d" % i, [4, 6, CW], BF16) for i in range(2)]
            R1 = [sb(es, "r1_%d" % i, [4, CW], F32) for i in range(2)]
            R2 = [sb(es, "r2_%d" % i, [4, CW], F32) for i in range(2)]
            for st in STK:
                OP("pool", lambda e, st=st: e.memset(st.ap, 1.0), w=[st])
            for ch in range(S // CW):
                st, r1, r2 = STK[ch % 2], R1[ch % 2], R2[ch % 2]
                c = cz[:, 1 + ch * CW:1 + (ch + 1) * CW]
                OP("dve", lambda e: e.tensor_scalar(out=st[:, 3, :], in0=c, scalar1=-1.0, scalar2=None, op0=ALU.mult), r=[cz], w=[st])
                OP("dve", lambda e: e.tensor_tensor(out=r1.ap, in0=c, in1=st[:, 3, :], op=ALU.add), r=[cz, st], w=[r1])
                OP("dve", lambda e: e.tensor_scalar(out=st[:, 4, :], in0=r1.ap, scalar1=-1.0, scalar2=None, op0=ALU.mult), r=[r1], w=[st])
                OP("dve", lambda e: e.tensor_tensor(out=r2.ap, in0=r1.ap, in1=st[:, 4, :], op=ALU.add), r=[r1, st], w=[r2])
                OP("dve", lambda e: e.tensor_scalar(out=st[:, 5, :], in0=r2.ap, scalar1=-1.0, scalar2=None, op0=ALU.mult), r=[r2], w=[st])
                DMA(CAK[:, :, ch * CW:(ch + 1) * CW], st.ap, r=[st])
            czb = cz[:, 0:S].rearrange("h (n i) -> h n i", i=128)[:, :, 0]
            refs = czb if NP == 1 else czb[:, 1::2]
            qh = sb(es, "qh", [4, NBO], BF16)
            ql = sb(es, "ql", [4, NBO], BF16)
            ql2 = sb(es, "ql2", [4, NBO], BF16)
            q1 = sb(es, "q1", [4, NBO], F32)
            q2 = sb(es, "q2", [4, NBO], F32)
            OP("dve", lambda e: e.tensor_copy(out=qh.ap, in_=refs), r=[cz], w=[qh])
            OP("dve", lambda e: e.tensor_tensor(out=q1.ap, in0=refs, in1=qh.ap, op=ALU.subtract), r=[cz, qh], w=[q1])
            OP("dve", lambda e: e.tensor_copy(out=ql.ap, in_=q1.ap), r=[q1], w=[ql])
            OP("dve", lambda e: e.tensor_tensor(out=q2.ap, in0=q1.ap, in1=ql.ap, op=ALU.subtract), r=[q1, ql], w=[q2])
            OP("dve", lambda e: e.tensor_copy(out=ql2.ap, in_=q2.ap), r=[q2], w=[ql2])
            NBC = min(8, NBO)
            STQ = [sb(es, "stq%d" % i, [4, 6, NBC * 128], BF16) for i in range(2)]
            for st in STQ:
                OP("pool", lambda e, st=st: e.memset(st.ap, 1.0), w=[st])
            for ch in range(NBO // NBC):
                st = STQ[ch % 2]
                for ri, src in enumerate((qh, ql, ql2)):
                    OP("dve", lambda e, ri=ri, src=src: e.tensor_copy(out=st[:, ri, :].rearrange("h (n i) -> h n i", i=128), in_=src[:, ch * NBC:(ch + 1) * NBC].unsqueeze(2).to_broadcast([4, NBC, 128])), r=[src], w=[st])
                DMA(CAQ[:, :, ch * NBC * 128:(ch + 1) * NBC * 128], st.ap, r=[st])
            S_.barrier()

    def qgroup_blocks(G):
        if NP == 1:
            return 4 * G, [(4 * G + d, d) for d in range(4)]
        return 8 * G, [(8 * G + d, d) for d in range(8)]

    def phase_b(l):
        with ExitStack() as es:
            KT_ = [sb(es, "KT%d" % i, [96, S], BF16) for i in range(2)]
            VV_ = [sb(es, "VV%d" % i, [128, NBLK, 65], BF16) for i in range(2)]
            QQ_ = [sb(es, "QQ%d" % i, [96, 512], BF16) for i in range(3)]
            QN_ = [sb(es, "QN%d" % i, [64, 512], BF16) for i in range(3)]
            PP_ = [sb(es, "PP%d" % i, [128, 512], BF16) for i in range(4)]
            EE_ = [sb(es, "EE%d" % i, [128, 512], F32) for i in range(3)]
            EX_ = [sb(es, "EX%d" % i, [128, 512], F32) for i in range(3)]
            SP_ = [sb(es, "SP%d" % i, [128, 512], BF16) for i in range(3)]
            OS_ = [sb(es, "OS%d" % i, [128, 4, 64], F32) for i in range(2)]
            RC_ = [sb(es, "RC%d" % i, [128, 4], F32) for i in range(2)]
            OA_ = [sb(es, "OA%d" % i, [128, 4, 64], F32) for i in range(2)]
            TMP_ = [sb(es, "TMPB%d" % i, [128, 4, 64], F32) for i in range(2)]
            RT_ = [sb(es, "RT%d" % i, [128, 4], F32) for i in range(2)]
            ER_ = [sb(es, "ER%d" % i, [128, 4], F32) for i in range(2)]
            for vv in VV_:
                OP("pool", lambda e, vv=vv: e.memset(vv.ap, 1.0), w=[vv])
            PO = [PB[0], PB[1]]
            PS = [PB[2], PB[3], PB[4], PB[5]]
            PVr = [T(PB[i][:, 0:256], "pvr%d" % i, psum=True, res=PB[i].res) for i in range(2)]
            TOTr = [T(PB[i][:, 256:260], "totr%d" % i, psum=True, res=PB[i].res) for i in range(2)]

            tasks = []
            for br in range(4):
                for h in range(4):
                    for G in range(NQG):
                        tasks.append({"br": br, "h": h, "G": G, "idx": len(tasks)})
            hb_of = {}

            def issue_head_loads(br, h):
                k = len(hb_of)
                hb_of[(br, h)] = k
                kt, vv = KT_[k % 2], VV_[k % 2]
                krow0 = (0, 288, 544)[br]
                load_cols_global(kt[0:64, :], KT_g, krow0 + h * 64, 64, 800, w=[kt])
                if br == 0:
                    load_cols_global(kt[64:96, :], KT_g, 256, 32, 800, w=[kt])
                if br == 2:
                    DMA(kt[64:70, :], CAK[h], w=[kt])
                load_v_global(vv[:, :, 0:64], br * 256 + h * 64, w=[vv])

            def issue_q_load(t):
                br, h, G = t["br"], t["h"], t["G"]
                qq = QQ_[t["idx"] % 3]
                t["qq"] = qq
                cols = slice(G * 512, (G + 1) * 512)
                if br == 3:
                    DMA(qq[0:64, :], QT[896 + h * 64:896 + (h + 1) * 64, cols], w=[qq])
                    return
                qrow0 = (0, 384, 640)[br]
                qstride = (96, 64, 64)[br]
                nq = 96 if br == 0 else 64
                DMA(qq[0:nq, :], QT[qrow0 + h * qstride:qrow0 + h * qstride + nq, cols], w=[qq])
                if br == 2:
                    DMA(qq[64:70, :], CAQ[h, :, cols], w=[qq])

            units = []
            for t in tasks:
                br, h, G = t["br"], t["h"], t["G"]
                if br == 3:
                    blocks = [(0, None), (1, None)]
                else:
                    nfull, diag = qgroup_blocks(G)
                    blocks = [(n, None) for n in range(nfull)] + diag
                    if br == 1:
                        blocks = blocks[::-1]
                for bi, (n, mid) in enumerate(blocks):
                    units.append({"t": t, "bi": bi, "n": n, "mid": mid, "last": bi == len(blocks) - 1, "u": len(units)})

            def first_of_task(u):
                return u["bi"] == 0

            def stage_prefetch(u):
                if not first_of_task(u):
                    return
                t = u["t"]
                ti = t["idx"]
                if ti == 0:
                    issue_head_loads(0, 0)
                    issue_q_load(tasks[0])
                if ti + 1 < len(tasks):
                    nt = tasks[ti + 1]
                    if nt["br"] < 3 and (nt["br"], nt["h"]) not in hb_of:
                        issue_head_loads(nt["br"], nt["h"])
                    issue_q_load(nt)

            def kv_of(t):
                k = hb_of[(t["br"], t["h"])]
                return KT_[k % 2], VV_[k % 2]

            def stage_s(u):
                stage_prefetch(u)
                t = u["t"]
                br, h, n, mid, qq = t["br"], t["h"], u["n"], u["mid"], t["qq"]
                if br == 3:
                    ps = PS[u["u"] % 4]
                    u["ps"] = ps
                    OP("pe", lambda e: e.matmul(ps.ap, lhsT=mkT[0:64, h, n * 128:(n + 1) * 128], rhs=qq[0:64, :], start=True, stop=True), r=[mkT, qq], w=[ps])
                elif br != 1:
                    kt, vv = kv_of(t)
                    dk = (96, 64, 70)[br]
                    ps = PS[u["u"] % 4]
                    u["ps"] = ps

                    if mid is None:
                        j0 = 0
                    elif NP == 1:
                        j0 = mid
                    else:
                        j0 = min(mid // 2, 3)
                    u["j0"] = j0
                    c0 = j0 * 128

                    def mms(e):
                        i = e.matmul(ps[:, c0:512], lhsT=kt[0:dk, n * 128:(n + 1) * 128], rhs=qq[0:dk, c0:512], start=True, stop=(mid is None))
                        if mid is not None:
                            i = e.matmul(ps[:, c0:512], lhsT=ident.ap, rhs=masks[:, mid, c0:512], start=False, stop=True)
                        return i
                    OP("pe", mms, r=[kt, qq, ident, masks], w=[ps])
                else:
                    kt, vv = kv_of(t)
                    pz = PS[u["u"] % 2]

                    def mmz(e):
                        i = e.matmul(pz.ap, lhsT=kt[0:64, n * 128:(n + 1) * 128], rhs=qq[0:64, :], start=True, stop=(mid is None))
                        if mid is not None:
                            i = e.matmul(pz.ap, lhsT=ident.ap, rhs=masks[:, NM + mid, :], start=False, stop=True)
                        return i
                    OP("pe", mmz, r=[kt, qq, ident, masks], w=[pz])
                    ee = EE_[u["u"] % 3]
                    u["ee"] = ee
                    spt = SP_[u["u"] % 3]
                    u["spt"] = spt
                    OP("act", lambda e: e.activation(out=ee.ap, in_=pz.ap, func=AF.Exp), r=[pz], w=[ee])
                    OP("act", lambda e: e.activation(out=spt.ap, in_=ee.ap, func=AF.Ln, bias=1.0), r=[ee], w=[spt])
                    return
                pp = PP_[u["u"] % 4]
                u["pp"] = pp
                c0 = u.get("j0", 0) * 128
                OP("act", lambda e: e.activation(out=pp[:, c0:512], in_=ps[:, c0:512], func=AF.Exp), r=[ps], w=[pp])

            def stage_l(u):
                t = u["t"]
                if t["br"] != 1:
                    return
                h, n, mid, spt = t["h"], u["n"], u["mid"], u["spt"]
                kt, vv = kv_of(t)
                pl = PS[2 + u["u"] % 2]
                totr = TOTr[u["u"] % 2]

                OP("pe", lambda e: e.matmul(pl.ap, lhsT=tri.ap, rhs=spt.ap, start=True, stop=True), r=[tri, spt], w=[pl])

                def mmt(e):
                    for j in range(4):
                        i = e.matmul(totr[:, j:j + 1], lhsT=spt[:, j * 128:(j + 1) * 128], rhs=onec.ap, start=True, stop=True, skip_group_check=True)
                    return i
                OP("pe", mmt, r=[spt, onec], w=[totr])
                aa = PP_[u["u"] % 4]
                u["pp"] = aa
                ex, ee = EX_[u["u"] % 3], u["ee"]
                OP("act", lambda e: e.activation(out=ex.ap, in_=pl.ap, func=AF.Exp, scale=-1.0), r=[pl], w=[ex])
                OP("pool", lambda e: e.tensor_tensor(out=aa.ap, in0=ee.ap, in1=ex.ap, op=ALU.mult), r=[ee, ex], w=[aa])

            def finalize_softmax(po, t):
                G = t["G"]
                col0 = t["br"] * 256 + t["h"] * 64
                rc, osb = RC_[t["idx"] % 2], OS_[t["idx"] % 2]
                po3 = po[:, 0:260].rearrange("p (j d) -> p j d", d=65)
                OP("dve", lambda e: e.reciprocal(out=rc.ap, in_=po3[:, :, 64]), r=[po], w=[rc])
                OP("dve", lambda e: e.tensor_tensor(out=osb.ap, in0=po3[:, :, 0:64], in1=rc.ap.unsqueeze(2).to_broadcast([128, 4, 64]), op=ALU.mult), r=[po, rc], w=[osb])
                DMA(Y[G * 512:(G + 1) * 512, col0:col0 + 64].rearrange("(j p) d -> p j d", p=128), osb.ap, r=[osb], q="pool")

            def stage_pv(u):
                t = u["t"]
                br, h, n, bi, pp = t["br"], t["h"], u["n"], u["bi"], u["pp"]
                if br != 1:
                    po = PO[t["idx"] % 2]
                    if br == 3:
                        rhs = mv[:, n, h, :]
                        rd = [pp, mv]
                    else:
                        kt, vv = kv_of(t)
                        rhs = vv[:, n, :]
                        rd = [pp, vv]

                    def mmo(e):
                        for j in range(u.get("j0", 0), 4):
                            i = e.matmul(po[:, j * 65:(j + 1) * 65], lhsT=pp[:, j * 128:(j + 1) * 128], rhs=rhs, start=(bi == 0 and j == 0), stop=u["last"], skip_group_check=True)
                        return i
                    OP("pe", mmo, r=rd, w=[po])
                    if u["last"]:
                        finalize_softmax(po, t)
                    return
                kt, vv = kv_of(t)
                pvr, totr = PVr[u["u"] % 2], TOTr[u["u"] % 2]
                oa, TMP, RT, ER = OA_[t["idx"] % 2], TMP_[t["idx"] % 2], RT_[t["idx"] % 2], ER_[t["idx"] % 2]

                def mmpv(e):
                    for j in range(4):
                        i = e.matmul(pvr[:, j * 64:(j + 1) * 64], lhsT=pp[:, j * 128:(j + 1) * 128], rhs=vv[:, n, 0:64], start=True, stop=True, skip_group_check=True)
                    return i
                OP("pe", mmpv, r=[pp, vv], w=[pvr])
                oaf = oa.ap.rearrange("p j d -> p (j d)")
                if bi == 0:
                    OP("dve", lambda e: e.tensor_copy(out=oaf, in_=pvr.ap), r=[pvr], w=[oa])
                    OP("dve", lambda e: e.tensor_copy(out=RT.ap, in_=totr.ap), r=[totr], w=[RT])
                else:
                    OP("act", lambda e: e.activation(out=ER.ap, in_=RT.ap, func=AF.Exp, scale=-1.0), r=[RT], w=[ER])
                    OP("dve", lambda e: e.tensor_tensor(out=TMP.ap, in0=pvr.ap.rearrange("p (j d) -> p j d", d=64), in1=ER.ap.unsqueeze(2).to_broadcast([128, 4, 64]), op=ALU.mult), r=[pvr, ER], w=[TMP])
                    OP("dve", lambda e: e.tensor_tensor(out=oa.ap, in0=oa.ap, in1=TMP.ap, op=ALU.add), r=[oa, TMP], w=[oa])
                    OP("dve", lambda e: e.tensor_tensor(out=RT.ap, in0=RT.ap, in1=totr.ap, op=ALU.add), r=[RT, totr], w=[RT])
                if u["last"]:
                    G = t["G"]
                    col0 = 256 + h * 64
                    DMA(Y[G * 512:(G + 1) * 512, col0:col0 + 64].rearrange("(j p) d -> p j d", p=128), oa.ap, r=[oa], q="pool")

            stages = [(stage_s, 0), (stage_l, 1), (stage_pv, 2)]
            nU = len(units)
            for tick in range(nU + 2):
                for f, sk in stages:
                    i = tick - sk
                    if 0 <= i < nU:
                        f(units[i])
            S_.barrier()

    def phase_c(l, last):
        with ExitStack() as es:
            wbr = sb(es, "wbr", [128, 8, D], BF16)
            DMA(wbr.ap, wbr_d[l].rearrange("n (c p) d -> p (n c) d", p=128), w=[wbr], q="pool")
            wmu = sb(es, "wmu", [128, 4096], BF16)
            DMA(wmu.ap, wmu_d[l], w=[wmu], q="pool")
            wo = sb(es, "wo", [128, 8, D], BF16)
            DMA(wo.ap, wout_d[l].rearrange("(c p) d -> p c d", p=128), w=[wo], q="pool")
            lng = sb(es, "lng", [128, D], F32)
            DMA(lng.ap, lng_d[l].partition_broadcast(128), w=[lng])
            lnb = sb(es, "lnb", [128, D], F32)
            DMA(lnb.ap, lnb_d[l].partition_broadcast(128), w=[lnb])

            def dbl(name, shape, dtype):
                return [sb(es, name + str(i), shape, dtype) for i in range(3)]
            def quad(name, shape, dtype):
                return [sb(es, name + str(i), shape, dtype) for i in range(4)]
            YT_ = quad("yt", [128, D], F32)
            GT_ = quad("gt", [128, D], F32)
            XT_ = quad("xc", [128, D], F32)
            MRT_ = quad("mrT", [128, 128], BF16)
            YB_ = dbl("yb", [128, D], BF16)
            YBT_ = dbl("ybT", [128, 8, 128], BF16)
            MG_ = dbl("mg", [128, D], F32)
            MER_ = dbl("mer", [128, D], F32)
            MB_ = dbl("mb", [128, D], BF16)
            MT_ = dbl("mT", [128, 8, 128], BF16)
            Z_ = dbl("z", [128, D], F32)
            ST_ = dbl("st", [128, 2, 6], F32)
            MV_ = dbl("mvar", [128, 2], F32)
            RS_ = dbl("rsd", [128, 1], F32)
            src = x_in if l == 0 else XR
            dst = y_out if last else XR
            def gen_c(tb):
                k2 = tb % 3
                k4 = tb % 4
                yt, gt, xt, mrT, yb, ybT, mg, mer, mb, mT, z, st, mvar, rsd = (
                    YT_[k4], GT_[k4], XT_[k4], MRT_[k4], YB_[k2], YBT_[k2], MG_[k2], MER_[k2], MB_[k2], MT_[k2],
                    Z_[k2], ST_[k2], MV_[k2], RS_[k2])
                rows = slice(tb * 128, (tb + 1) * 128)
                DMA(yt.ap, Y[rows, :], w=[yt])
                DMA(gt.ap, GZ[rows, :], w=[gt])
                DMA(xt.ap, src[rows, :], w=[xt])
                DMA(mrT.ap, MR[:, rows], w=[mrT])
                yield
                OP("pool", lambda e: e.tensor_tensor(out=yb.ap, in0=yt.ap, in1=gt.ap, op=ALU.mult), r=[yt, gt], w=[yb])
                pt = ptbank()

                def tr8(src_t, pt):
                    def f(e):
                        for c in range(8):
                            i = e.transpose(out=pt[:, c * 128:(c + 1) * 128], in_=src_t[:, c * 128:(c + 1) * 128], identity=ident.ap)
                        return i
                    return f
                OP("pe", tr8(yb, pt), r=[yb, ident], w=[pt])
                OP("act", lambda e, pt=pt: e.copy(out=ybT.ap.rearrange("p c t -> p (c t)"), in_=pt.ap), r=[pt], w=[ybT])
                for n in range(4):
                    for hf in range(2):
                        pbr = pbank()
                        pg = pbank()
                        hs = slice(hf * 512, (hf + 1) * 512)

                        def mmb(e, n=n, hs=hs, pbr=pbr):
                            for c in range(2):
                                i = e.matmul(pbr.ap, lhsT=ybT[:, 2 * n + c, :], rhs=wbr[:, 2 * n + c, hs], start=(c == 0), stop=(c == 1))
                            return i
                        OP("pe", mmb, r=[ybT, wbr], w=[pbr])
                        OP("pe", lambda e, n=n, hf=hf, pg=pg: e.matmul(pg.ap, lhsT=mrT.ap, rhs=wmu[:, n * 1024 + hf * 512:n * 1024 + (hf + 1) * 512], start=True, stop=True), r=[mrT, wmu], w=[pg])
                        OP("act", lambda e, pg=pg, hs=hs: e.activation(out=mg[:, hs], in_=pg.ap, func=AF.Sigmoid), r=[pg], w=[mg])
                        if n == 0:
                            OP("dve", lambda e, pbr=pbr, hs=hs: e.tensor_tensor(out=mer[:, hs], in0=mg[:, hs], in1=pbr.ap, op=ALU.mult), r=[mg, pbr], w=[mer])
                        else:
                            OP("dve", lambda e, pbr=pbr, hs=hs: e.tensor_tensor(out=mg[:, hs], in0=mg[:, hs], in1=pbr.ap, op=ALU.mult), r=[mg, pbr], w=[mg])
                            OP("pool", lambda e, hs=hs: e.tensor_tensor(out=mer[:, hs], in0=mer[:, hs], in1=mg[:, hs], op=ALU.add), r=[mer, mg], w=[mer])
                OP("pool", lambda e: e.tensor_copy(out=mb.ap, in_=mer.ap), r=[mer], w=[mb])
                yield
                pt = ptbank()
                OP("pe", tr8(mb, pt), r=[mb, ident], w=[pt])
                OP("act", lambda e, pt=pt: e.copy(out=mT.ap.rearrange("p c t -> p (c t)"), in_=pt.ap), r=[pt], w=[mT])
                for hf in range(2):
                    po = pbank()
                    hs = slice(hf * 512, (hf + 1) * 512)

                    def mmo(e, hs=hs, po=po):
                        for c in range(8):
                            i = e.matmul(po.ap, lhsT=mT[:, c, :], rhs=wo[:, c, hs], start=(c == 0), stop=(c == 7))
                        return i
                    OP("pe", mmo, r=[mT, wo], w=[po])
                    OP("dve", lambda e, hs=hs, po=po: e.scalar_tensor_tensor(out=z[:, hs], in0=xt[:, hs], scalar=float(depth_alpha), in1=po.ap, op0=ALU.mult, op1=ALU.add), r=[xt, po], w=[z])
                    OP("dve", lambda e, hs=hs, hf=hf: e.bn_stats(out=st[:, hf, :], in_=z[:, hs]), r=[z], w=[st])
                OP("dve", lambda e: e.bn_aggr(out=mvar.ap, in_=st.ap.rearrange("p a s -> p (a s)")), r=[st], w=[mvar])
                OP("dve", lambda e: e.tensor_scalar(out=rsd.ap, in0=mvar[:, 1:2], scalar1=LN_EPS, scalar2=None, op0=ALU.add), r=[mvar], w=[rsd])
                OP("act", lambda e: e.activation(out=rsd.ap, in_=rsd.ap, func=AF.Sqrt), r=[rsd], w=[rsd])
                OP("dve", lambda e: e.reciprocal(out=rsd.ap, in_=rsd.ap), r=[rsd], w=[rsd])
                yield
                OP("dve", lambda e: e.tensor_scalar(out=z.ap, in0=z.ap, scalar1=mvar[:, 0:1], scalar2=rsd.ap, op0=ALU.subtract, op1=ALU.mult), r=[z, mvar, rsd], w=[z])
                OP("pool", lambda e: e.tensor_tensor(out=z.ap, in0=z.ap, in1=lng.ap, op=ALU.mult), r=[z, lng], w=[z])
                OP("pool", lambda e: e.tensor_tensor(out=z.ap, in0=z.ap, in1=lnb.ap, op=ALU.add), r=[z, lnb], w=[z])
                t = DMA(dst[rows, :], z.ap, r=[z])
                if last:
                    S_.out_tickets.append(t)


            gens = [gen_c(tb) for tb in range(NBO)]
            for tick in range(NBO + 3):
                for sk in range(4):
                    i = tick - sk
                    if 0 <= i < NBO:
                        next(gens[i], None)
            S_.barrier()

    def gather(l):
        if NP == 1:
            return
        groups = [[2 * i, 2 * i + 1] for i in range(4)]
        pairs = [(KT_in[r0:r0 + n, :], KT_g[2 * r0:2 * r0 + 2 * n, :]) for r0, n in KT_CH]
        pairs += [(V_in[c * VCH:(c + 1) * VCH, :], V_g[2 * c * VCH:2 * (c + 1) * VCH, :]) for c in range(NOWN // VCH)]
        pairs += [(LF_in, LF_g)]
        for src_t, dst_t in pairs:
            ccs["n"] += 1
            nc.gpsimd.collective_compute("AllGather", ALU.bypass, replica_groups=groups, ins=[src_t.opt()], outs=[dst_t.opt()]).then_inc(ccs["sem"], 1)
        for e in ("pe", "act", "dve", "pool", "sp"):
            S_.eng[e].wait_ge(ccs["sem"], ccs["n"])
        S_.barrier()

    ccs = {"n": 0, "sem": nc.alloc_semaphore("cc_sem") if NP > 1 else None}
    import os
    stop = os.environ.get("KSTOP", "")
    for l in range(L):
        if stop == "setup":
            break
        phase_a(l)
        if stop == "a":
            break
        gather(l)
        phase_fc(l)
        if stop == "fc":
            break
        phase_b(l)
        if stop == "b":
            break
        phase_c(l, l == L - 1)
    for t in S_.out_tickets:
        S_._wait("sp", t)
    ges.close()
    return nc


def make_masks(NP, rank):
    NM = 4 if NP == 1 else 8
    qblk = (0, 1, 2, 3) if NP == 1 else BLK2[rank]
    m = np.zeros((128, 2 * NM, 512), np.float32)
    k = np.arange(128)[:, None]
    i = np.arange(128)[None, :]
    for strict in range(2):
        for d in range(NM):
            for j, qb in enumerate(qblk):
                if d < qb:
                    blk = np.zeros((128, 128), np.float32)
                elif d > qb:
                    blk = np.full((128, 128), NEG, np.float32)
                else:
                    ok = (k < i) if strict else (k <= i)
                    blk = np.where(ok, 0.0, NEG).astype(np.float32)
                m[:, strict * NM + d, j * 128:(j + 1) * 128] = blk
    return m


def own_blocks(S, NP, rank):
    if NP == 1:
        return list(range(S // 128))
    return [8 * g + m for g in range(S // 1024) for m in BLK2[rank]]


_PROG_CACHE = {}


def run(inputs, S, L, NP):
    key = (S, L, NP)
    alpha = (2 * L) ** 0.25
    if key not in _PROG_CACHE:
        _PROG_CACHE[key] = build_program(S, L, NP, alpha)
    nc = _PROG_CACHE[key]
    x = np.asarray(inputs["x"], np.float32)
    B = x.shape[0]
    in_maps = []
    meta = []
    wnames = ["w_in", "mla_q_norm", "mla_w_qb", "mla_kv_norm", "mla_w_kvb", "fox_forget_bias", "w_mem_kv", "w_merge_up",
              "w_branch", "w_out", "ln_gain", "ln_bias"]
    w = {k: np.ascontiguousarray(np.asarray(inputs[k], np.float32)) for k in wnames}
    pos = np.asarray(inputs["positions"]).astype(np.int32)
    for core in range(8):
        if NP == 1:
            b, rank = core % B, 0
        else:
            b, rank = core // 2, core % 2
        blks = own_blocks(S, NP, rank)
        tok = np.concatenate([np.arange(n * 128, (n + 1) * 128) for n in blks])
        m = dict(w)
        m["x"] = np.ascontiguousarray(x[b][tok])
        m["mem"] = np.ascontiguousarray(np.asarray(inputs["mem"], np.float32)[b])
        m["pos"] = np.ascontiguousarray(pos[b][tok].reshape(len(blks), 128).T)
        m["masks"] = make_masks(NP, rank)
        in_maps.append(m)
        meta.append((b, tok))
    res = run_bass_kernel_spmd(nc, in_maps, core_ids=list(range(8)))
    out = np.zeros_like(x)
    ncores = 8 if NP == 2 else B
    for core in range(ncores):
        b, tok = meta[core]
        out[b][tok] = res.results[core]["y"]
    return out


def kernel(**inputs):
    return run(inputs, 8192, 4, 2)
```

```python
import numpy as np
import concourse.bass as bass
import concourse.mybir as mybir
from concourse.bass_utils import run_bass_kernel_spmd

F32 = mybir.dt.float32
BF16 = mybir.dt.bfloat16
I32 = mybir.dt.int32
AF = mybir.ActivationFunctionType
ALU = mybir.AluOpType
AX = mybir.AxisListType


class Res:
    __slots__ = ("name", "w", "r")

    def __init__(self, name):
        self.name = name
        self.w = None
        self.r = {}


class Sched:
    def __init__(self, nc, n_dma_sems=40):
        self.nc = nc
        self.eng = {"pe": nc.tensor, "act": nc.scalar, "dve": nc.vector, "pool": nc.gpsimd, "sp": nc.sync}
        self.sem = {k: nc.alloc_semaphore("prog_" + k) for k in ("pe", "act", "dve", "pool")}
        self.cnt = {k: 0 for k in self.sem}
        self.dsem = [nc.alloc_semaphore("dma%d" % i) for i in range(n_dma_sems)]
        self.dval = [0] * n_dma_sems
        self.dnext = 0
        self.dnext_q = {"sp": 0, "pool": n_dma_sems - 8}
        self.drange = {"sp": (0, n_dma_sems - 8), "pool": (n_dma_sems - 8, n_dma_sems)}
        self.waited = {k: {} for k in self.eng}
        self.out_tickets = []

    def _wait(self, e, ticket):
        if ticket is None:
            return
        key, val = ticket
        if key == "pe" and e == "pe":
            return
        w = self.waited[e]
        if w.get(key, 0) >= val:
            return
        w[key] = val
        sem = self.sem[key] if isinstance(key, str) else self.dsem[key]
        self.eng[e].wait_ge(sem, val)

    def _deps(self, e, reads, writes):
        for r in reads:
            self._wait(e, r.w)
        for r in writes:
            self._wait(e, r.w)
            for k, v in r.r.items():
                self._wait(e, (k, v))

    def _commit(self, ticket, reads, writes):
        for r in reads:
            if r.r.get(ticket[0], 0) < ticket[1]:
                r.r[ticket[0]] = ticket[1]
        for r in writes:
            r.w = ticket
            r.r = {}

    def op(self, e, fn, reads=(), writes=()):
        self._deps(e, reads, writes)
        ins = fn(self.eng[e])
        self.cnt[e] += 1
        ins.then_inc(self.sem[e], 1)
        t = (e, self.cnt[e])
        self._commit(t, reads, writes)
        return t

    def dma(self, out, in_, reads=(), writes=(), q="sp"):
        i = self.dnext_q[q]
        lo, hi = self.drange[q]
        self.dnext_q[q] = lo + (i + 1 - lo) % (hi - lo)
        if self.dval[i]:
            self._wait(q, (i, self.dval[i]))
        self._deps(q, reads, writes)
        self.dval[i] += 16
        self.eng[q].dma_start(out=out, in_=in_).then_inc(self.dsem[i], 16)
        t = (i, self.dval[i])
        self._commit(t, reads, writes)
        return t

    def barrier(self):
        for e in ("pe", "act", "dve", "pool", "sp"):
            for k in self.sem:
                if self.cnt[k]:
                    self._wait(e, (k, self.cnt[k]))
            for i, v in enumerate(self.dval):
                if v:
                    self._wait(e, (i, v))

    def finish(self, resources):
        for r in resources:
            self._wait("sp", r.w)


class T:
    def __init__(self, ap, name, psum=False, res=None):
        self.ap = ap
        self.res = res if res is not None else Res(name)
        self.psum = psum

    def __getitem__(self, idx):
        return self.ap[idx]


D_MODEL = 1024
IN_W = 3620
BLK2 = ((0, 3, 4, 7), (1, 2, 5, 6))
NEG = -30000.0
RMS_EPS = 1e-6
LN_EPS = 1e-5
WSEG = {"cq": (0, 384), "ckv": (384, 256), "kr": (640, 32), "sbq": (672, 256), "sbk": (928, 256), "sbv": (1184, 256),
        "fq": (1440, 256), "fk": (1696, 256), "fv": (1952, 256), "ff": (2208, 4), "mq": (2212, 256),
        "gz": (2468, 1024), "mr": (3492, 128)}
WORDER = ["cq", "ckv", "kr", "sbq", "fq", "mq", "sbk", "fk", "mr", "ff", "sbv", "fv", "gz"]
WOFF = {}
_o = 0
for _k in WORDER:
    WOFF[_k] = _o
    _o += WSEG[_k][1]
assert _o == IN_W


def kv_place(n, NP):
    if NP == 1:
        return 0, n
    g, m = divmod(n, 8)
    r = 0 if m in BLK2[0] else 1
    return r, g * 4 + BLK2[r].index(m)


def build_program(S, L, NP, depth_alpha):
    nc = bass.Bass("TRN2", target_bir_lowering=False)
    D = D_MODEL
    NOWN = S // NP
    NBO = NOWN // 128
    NBLK = S // 128
    NQG = NOWN // 512
    NSB = S // 1024
    NM = 4 if NP == 1 else 8
    dt = nc.dram_tensor

    def din(name, shape, dtype=F32):
        return dt(name, list(shape), dtype, kind="ExternalInput").ap()

    x_in = din("x", [NOWN, D])
    mem_in = din("mem", [256, D])
    pos_in = din("pos", [128, NBO], I32)
    masks_in = din("masks", [128, 2 * NM, 512])
    w_in_d = din("w_in", [L, D, IN_W])
    qn_d = din("mla_q_norm", [L, 384])
    wqb_d = din("mla_w_qb", [L, 384, 384])
    kvn_d = din("mla_kv_norm", [L, 256])
    wkvb_d = din("mla_w_kvb", [L, 256, 512])
    fb_d = din("fox_forget_bias", [L, 4])
    wmkv_d = din("w_mem_kv", [L, D, 512])
    wmu_d = din("w_merge_up", [L, 128, 4096])
    wbr_d = din("w_branch", [L, 4, 256, D])
    wout_d = din("w_out", [L, D, D])
    lng_d = din("ln_gain", [L, D])
    lnb_d = din("ln_bias", [L, D])
    y_out = dt("y", [NOWN, D], F32, kind="ExternalOutput").ap()

    XR = dt("XR", [NOWN, D], F32).ap()
    KT_in = dt("KT_in", [800, NOWN], BF16).ap()
    V_in = dt("V_in", [NOWN, 768], BF16).ap()
    LF_in = dt("LF_in", [4, NOWN], F32).ap()
    if NP == 1:
        KT_g, V_g, LF_g = KT_in, V_in, LF_in
    else:
        KT_g = dt("KT_g", [NP * 800, NOWN], BF16).ap()
        V_g = dt("V_g", [NP * NOWN, 768], BF16).ap()
        LF_g = dt("LF_g", [NP * 4, NOWN], F32).ap()
    QT = dt("QT", [1152, NOWN], BF16).ap()
    GZ = dt("GZ", [NOWN, D], F32).ap()
    MR = dt("MR", [128, NOWN], BF16).ap()
    Y = dt("Y", [NOWN, D], F32).ap()
    CAK = dt("CAK", [4, 6, S], BF16).ap()
    CAQ = dt("CAQ", [4, 6, NOWN], BF16).ap()

    S_ = Sched(nc)

    import os
    KMAX = int(os.environ.get("KMAX", "1000000000"))
    nops = [0]

    def OP(e, fn, r=(), w=()):
        nops[0] += 1
        if nops[0] > KMAX:
            return None
        return S_.op(e, fn, [t.res for t in r if not t.psum], [t.res for t in w] + [t.res for t in r if t.psum])

    def DMA(out, in_, r=(), w=(), q="sp"):
        nops[0] += 1
        if nops[0] > KMAX:
            return None
        if nops[0] == KMAX:
            print("LAST DMA", q, out.shape, in_.shape)
        return S_.dma(out, in_, [t.res for t in r], [t.res for t in w], q)

    from contextlib import ExitStack

    uid = [0]

    def sb(es, name, shape, dtype):
        uid[0] += 1
        nm = "s%d_%s" % (uid[0], name)
        return T(es.enter_context(nc.sbuf_tensor(nm, list(shape), dtype)).ap(), nm)

    PB = [T(nc.alloc_psum_tensor("pb%d" % i, [128, 512], F32).ap(), "pb%d" % i, psum=True) for i in range(6)]
    PT = [T(nc.alloc_psum_tensor("pt%d" % i, [128, 1024], BF16).ap(), "pt%d" % i, psum=True) for i in range(2)]
    rr = {"pb": 0, "pt": 0}

    def pbank():
        rr["pb"] = (rr["pb"] + 1) % 6
        return PB[rr["pb"]]

    def ptbank():
        rr["pt"] = (rr["pt"] + 1) % 2
        return PT[rr["pt"]]

    ges = ExitStack()
    ident = sb(ges, "ident", [128, 128], BF16)
    nident = sb(ges, "nident", [128, 128], BF16)
    tri = sb(ges, "tri", [128, 128], BF16)
    onec = sb(ges, "onec", [128, 1], BF16)
    masks = sb(ges, "masks", [128, 2 * NM, 512], BF16)
    rope = sb(ges, "rope", [128, NBO, 48], F32)
    ropeq = sb(ges, "ropeq", [128, NBO, 48], F32)
    memT = sb(ges, "memT", [128, 8, 256], BF16)
    mkT = sb(ges, "mkT", [64, 4, 256], BF16)
    mv = sb(ges, "mv", [128, 2, 4, 65], BF16)

    OP("pool", lambda e: e.memset(ident.ap, 0.0), w=[ident])
    OP("pool", lambda e: e.affine_select(out=ident.ap, in_=ident.ap, pattern=[[-1, 128]], compare_op=ALU.not_equal,
                                         fill=1.0, base=0, channel_multiplier=1), r=[ident], w=[ident])
    OP("pool", lambda e: e.memset(nident.ap, 0.0), w=[nident])
    OP("pool", lambda e: e.affine_select(out=nident.ap, in_=nident.ap, pattern=[[-1, 128]], compare_op=ALU.not_equal,
                                         fill=-1.0, base=0, channel_multiplier=1), r=[nident], w=[nident])
    OP("pool", lambda e: e.memset(tri.ap, 1.0), w=[tri])
    OP("pool", lambda e: e.affine_select(out=tri.ap, in_=tri.ap, pattern=[[-1, 128]], compare_op=ALU.is_ge,
                                         fill=0.0, base=0, channel_multiplier=1), r=[tri], w=[tri])
    OP("pool", lambda e: e.memset(onec.ap, 1.0), w=[onec])
    OP("pool", lambda e: e.memset(mv.ap, 1.0), w=[mv])

    with ExitStack() as es:
        mstage = sb(es, "mstage", [128, 2 * NM, 512], F32)
        DMA(mstage.ap, masks_in, w=[mstage])
        OP("dve", lambda e: e.tensor_copy(out=masks.ap, in_=mstage.ap), r=[mstage], w=[masks])
        posi = sb(es, "posi", [128, NBO], I32)
        posf = sb(es, "posf", [128, NBO], F32)
        invf = sb(es, "invf", [128, 16], F32)
        ang = sb(es, "ang", [128, NBO, 16], F32)
        tq = sb(es, "tq", [128, NBO, 16], F32)
        ti = sb(es, "ti", [128, NBO, 16], I32)
        rr_ = sb(es, "rr_", [128, NBO, 16], F32)
        rc_ = sb(es, "rc_", [128, NBO, 16], F32)
        mk_ = sb(es, "mk_", [128, NBO, 16], F32)
        DMA(posi.ap, pos_in, w=[posi])
        OP("dve", lambda e: e.tensor_copy(out=posf.ap, in_=posi.ap), r=[posi], w=[posf])
        for i in range(16):
            v = float(np.float32(10000.0) ** np.float32(-(2.0 * i) / 32.0))
            OP("pool", lambda e, i=i, v=v: e.memset(invf[:, i:i + 1], v), w=[invf])
        OP("dve", lambda e: e.tensor_tensor(out=ang.ap, in0=posf.ap.unsqueeze(2).to_broadcast([128, NBO, 16]),
                                            in1=invf.ap.unsqueeze(1).to_broadcast([128, NBO, 16]), op=ALU.mult),
           r=[posf, invf], w=[ang])
        TWO_PI = 2.0 * np.pi
        C1 = 6.28125
        C2 = TWO_PI - C1
        OP("dve", lambda e: e.tensor_scalar(out=tq.ap, in0=ang.ap, scalar1=float(1.0 / TWO_PI), scalar2=None, op0=ALU.mult), r=[ang], w=[tq])
        OP("dve", lambda e: e.tensor_copy(out=ti.ap, in_=tq.ap), r=[tq], w=[ti])
        OP("dve", lambda e: e.tensor_copy(out=tq.ap, in_=ti.ap), r=[ti], w=[tq])
        OP("dve", lambda e: e.scalar_tensor_tensor(out=rr_.ap, in0=tq.ap, scalar=-C1, in1=ang.ap, op0=ALU.mult, op1=ALU.add), r=[tq, ang], w=[rr_])
        OP("dve", lambda e: e.scalar_tensor_tensor(out=rr_.ap, in0=tq.ap, scalar=-C2, in1=rr_.ap, op0=ALU.mult, op1=ALU.add), r=[tq, rr_], w=[rr_])

        def wrap(t):
            OP("dve", lambda e: e.tensor_scalar(out=mk_.ap, in0=t.ap, scalar1=float(np.pi), scalar2=float(-TWO_PI), op0=ALU.is_gt, op1=ALU.mult), r=[t], w=[mk_])
            OP("dve", lambda e: e.tensor_tensor(out=t.ap, in0=t.ap, in1=mk_.ap, op=ALU.add), r=[t, mk_], w=[t])
            OP("dve", lambda e: e.tensor_scalar(out=mk_.ap, in0=t.ap, scalar1=float(-np.pi), scalar2=float(TWO_PI), op0=ALU.is_lt, op1=ALU.mult), r=[t], w=[mk_])
            OP("dve", lambda e: e.tensor_tensor(out=t.ap, in0=t.ap, in1=mk_.ap, op=ALU.add), r=[t, mk_], w=[t])
            OP("dve", lambda e: e.tensor_scalar(out=t.ap, in0=t.ap, scalar1=3.1415925, scalar2=-3.1415925, op0=ALU.min, op1=ALU.max), r=[t], w=[t])

        wrap(rr_)
        OP("dve", lambda e: e.tensor_scalar(out=rc_.ap, in0=rr_.ap, scalar1=float(np.pi / 2), scalar2=None, op0=ALU.add), r=[rr_], w=[rc_])
        wrap(rc_)
        OP("act", lambda e: e.activation(out=rope[:, :, 0:16], in_=rc_.ap, func=AF.Sin), r=[rc_], w=[rope])
        OP("act", lambda e: e.activation(out=rope[:, :, 16:32], in_=rr_.ap, func=AF.Sin), r=[rr_], w=[rope])
        OP("act", lambda e: e.activation(out=rope[:, :, 32:48], in_=rc_.ap, func=AF.Sin), r=[rc_], w=[rope])
        OP("dve", lambda e: e.tensor_scalar(out=ropeq.ap, in0=rope.ap, scalar1=float(96.0 ** -0.5), scalar2=None, op0=ALU.mult), r=[rope], w=[ropeq])
        mst = sb(es, "mst", [128, 2, D], F32)
        mbf = sb(es, "mbf", [128, 2, D], BF16)
        DMA(mst.ap, mem_in.rearrange("(c p) d -> p c d", p=128), w=[mst])
        OP("dve", lambda e: e.tensor_copy(out=mbf.ap, in_=mst.ap), r=[mst], w=[mbf])
        for mc in range(2):
            pt = ptbank()

            def trm(e, mc=mc, pt=pt):
                for c in range(8):
                    i = e.transpose(out=pt[:, c * 128:(c + 1) * 128], in_=mbf[:, mc, c * 128:(c + 1) * 128], identity=ident.ap)
                return i
            OP("pe", trm, r=[mbf, ident], w=[pt])
            OP("act", lambda e, mc=mc, pt=pt: e.copy(out=memT[:, :, mc * 128:(mc + 1) * 128],
                                                     in_=pt.ap.rearrange("p (c t) -> p c t", t=128)), r=[pt], w=[memT])
        S_.barrier()
        print("nops after setup", nops[0])

    F0 = WOFF["sbq"]

    def phase_a(l):
        with ExitStack() as es:
            win = sb(es, "win", [128, 8, IN_W], BF16)
            for k in WORDER:
                c0, wd = WSEG[k]
                DMA(win[:, :, WOFF[k]:WOFF[k] + wd], w_in_d[l][:, c0:c0 + wd].rearrange("(c p) n -> p c n", p=128), w=[win], q="pool")
            wqb = sb(es, "wqb", [128, 3, 384], BF16)
            DMA(wqb.ap, wqb_d[l].rearrange("(c p) n -> p c n", p=128), w=[wqb], q="pool")
            wkvb = sb(es, "wkvb", [128, 2, 512], BF16)
            DMA(wkvb.ap, wkvb_d[l].rearrange("(c p) n -> p c n", p=128), w=[wkvb], q="pool")
            wmkv = sb(es, "wmkv", [128, 8, 512], BF16)
            DMA(wmkv.ap, wmkv_d[l].rearrange("(c p) n -> p c n", p=128), w=[wmkv], q="pool")
            gq = sb(es, "gq", [128, 384], F32)
            DMA(gq.ap, qn_d[l].partition_broadcast(128), w=[gq])
            gkv = sb(es, "gkv", [128, 256], F32)
            DMA(gkv.ap, kvn_d[l].partition_broadcast(128), w=[gkv])
            fbn = sb(es, "fbn", [4, 1], F32)
            DMA(fbn.ap, fb_d[l:l + 1, :].rearrange("o h -> h o"), w=[fbn])
            OP("dve", lambda e: e.tensor_scalar(out=fbn.ap, in0=fbn.ap, scalar1=-1.0, scalar2=None, op0=ALU.mult), r=[fbn], w=[fbn])
            for h in range(4):
                pb = pbank()

                def mmk(e, h=h, pb=pb):
                    for c in range(8):
                        i = e.matmul(pb[0:64, 0:256], lhsT=wmkv[:, c, h * 64:(h + 1) * 64], rhs=memT[:, c, :], start=(c == 0), stop=(c == 7))
                    return i
                OP("pe", mmk, r=[wmkv, memT], w=[pb])
                OP("act", lambda e, h=h, pb=pb: e.copy(out=mkT[:, h, :], in_=pb[0:64, 0:256]), r=[pb], w=[mkT])
            for mc in range(2):
                pb = pbank()

                def mmv(e, mc=mc, pb=pb):
                    for c in range(8):
                        i = e.matmul(pb[:, 0:256], lhsT=memT[:, c, mc * 128:(mc + 1) * 128], rhs=wmkv[:, c, 256:512], start=(c == 0), stop=(c == 7))
                    return i
                OP("pe", mmv, r=[wmkv, memT], w=[pb])
                OP("act", lambda e, mc=mc, pb=pb: e.copy(out=mv[:, mc, :, 0:64], in_=pb[:, 0:256].rearrange("p (h d) -> p h d", d=64)), r=[pb], w=[mv])

            def dbl(name, shape, dtype):
                return [sb(es, name + str(i), shape, dtype) for i in range(3)]
            XT_ = [sb(es, "xt%d" % i, [128, D], F32) for i in range(4)]
            XB_ = dbl("xb", [128, D], BF16)
            XTT_ = dbl("xT", [128, 8, 128], BF16)
            JK_ = dbl("junk", [128, 384], F32)
            SS_ = dbl("ss", [128, 2], F32)
            RS_ = dbl("rs", [128, 2], F32)
            CN_ = dbl("cn", [128, 768], BF16)
            T4_ = dbl("t4", [128, 64], F32)
            CT_ = dbl("cT", [128, 6, 128], BF16)
            T8_ = dbl("t8", [128, 256], F32)
            QF_ = dbl("qf", [128, 384], BF16)
            QT_ = dbl("qT", [128, 4, 128], BF16)
            KN_ = dbl("kn", [128, 256], BF16)
            KNT_ = dbl("knT", [128, 2, 128], BF16)
            VT_ = dbl("vt", [128, 768], BF16)
            FT_ = dbl("FT", [128, 11, 128], BF16)
            LFA_ = dbl("lfa", [4, 128], F32)
            LFO_ = dbl("lfo", [4, 128], F32)
            GZT_ = dbl("gzt", [128, D], F32)
            src = x_in if l == 0 else XR
            def gen_a(tb):
                k2 = tb % 3
                xt, xb, xT, junk, ss, rs, cn, t4, cT, t8, qf, qT, kn, knT, vt, FT, lfa, lfo, gzt = (
                    XT_[tb % 4], XB_[k2], XTT_[k2], JK_[k2], SS_[k2], RS_[k2], CN_[k2], T4_[k2], CT_[k2], T8_[k2], QF_[k2],
                    QT_[k2], KN_[k2], KNT_[k2], VT_[k2], FT_[k2], LFA_[k2], LFO_[k2], GZT_[k2])
                rows = slice(tb * 128, (tb + 1) * 128)
                DMA(xt.ap, src[rows, :], w=[xt])
                yield
                OP("pool", lambda e: e.tensor_copy(out=xb.ap, in_=xt.ap), r=[xt], w=[xb])
                pt = ptbank()

                def trx(e, pt=pt, xb=xb):
                    for c in range(8):
                        i = e.transpose(out=pt[:, c * 128:(c + 1) * 128], in_=xb[:, c * 128:(c + 1) * 128], identity=ident.ap)
                    return i
                OP("pe", trx, r=[xb, ident], w=[pt])
                OP("act", lambda e, pt=pt, xT=xT: e.copy(out=xT.ap.rearrange("p c t -> p (c t)"), in_=pt.ap), r=[pt], w=[xT])

                def proj_tok(pb, n, woff, xT=xT):
                    def f(e):
                        for c in range(8):
                            i = e.matmul(pb[:, 0:n], lhsT=xT[:, c, :], rhs=win[:, c, woff:woff + n], start=(c == 0), stop=(c == 7))
                        return i
                    return f
                pa = pbank()
                OP("pe", proj_tok(pa, 384, WOFF["cq"]), r=[xT, win], w=[pa])
                pk = pbank()
                OP("pe", proj_tok(pk, 288, WOFF["ckv"]), r=[xT, win], w=[pk])
                OP("act", lambda e, pa=pa, junk=junk, ss=ss: e.activation(out=junk[:, 0:384], in_=pa[:, 0:384], func=AF.Square, accum_out=ss[:, 0:1]), r=[pa], w=[junk, ss])
                OP("act", lambda e, pk=pk, junk=junk, ss=ss: e.activation(out=junk[:, 0:256], in_=pk[:, 0:256], func=AF.Square, accum_out=ss[:, 1:2]), r=[pk], w=[junk, ss])
                OP("dve", lambda e, ss=ss, rs=rs: e.tensor_scalar(out=rs[:, 0:1], in0=ss[:, 0:1], scalar1=1.0 / 384, scalar2=RMS_EPS, op0=ALU.mult, op1=ALU.add), r=[ss], w=[rs])
                OP("dve", lambda e, ss=ss, rs=rs: e.tensor_scalar(out=rs[:, 1:2], in0=ss[:, 1:2], scalar1=1.0 / 256, scalar2=RMS_EPS, op0=ALU.mult, op1=ALU.add), r=[ss], w=[rs])
                OP("act", lambda e, rs=rs: e.activation(out=rs.ap, in_=rs.ap, func=AF.Sqrt), r=[rs], w=[rs])
                OP("dve", lambda e, rs=rs: e.reciprocal(out=rs.ap, in_=rs.ap), r=[rs], w=[rs])
                OP("dve", lambda e, pa=pa, rs=rs, cn=cn: e.scalar_tensor_tensor(out=cn[:, 0:384], in0=pa[:, 0:384], scalar=rs[:, 0:1], in1=gq.ap, op0=ALU.mult, op1=ALU.mult), r=[pa, rs, gq], w=[cn])
                OP("dve", lambda e, pk=pk, rs=rs, cn=cn: e.scalar_tensor_tensor(out=cn[:, 384:640], in0=pk[:, 0:256], scalar=rs[:, 1:2], in1=gkv.ap, op0=ALU.mult, op1=ALU.mult), r=[pk, rs, gkv], w=[cn])
                kr2 = pk[:, 256:288].rearrange("p (a i) -> p a i", a=2)
                OP("dve", lambda e, t4=t4, kr2=kr2: e.tensor_tensor(out=t4[:, 0:32].rearrange("p (a i) -> p a i", a=2), in0=kr2, in1=rope[:, tb, 0:32].rearrange("p (a i) -> p a i", a=2), op=ALU.mult), r=[pk, rope], w=[t4])
                OP("dve", lambda e, t4=t4, kr2=kr2: e.tensor_tensor(out=t4[:, 32:64].rearrange("p (a i) -> p a i", a=2), in0=kr2, in1=rope[:, tb, 16:48].rearrange("p (a i) -> p a i", a=2), op=ALU.mult), r=[pk, rope], w=[t4])
                OP("dve", lambda e, t4=t4, cn=cn: e.tensor_tensor(out=cn[:, 640:656], in0=t4[:, 0:16], in1=t4[:, 16:32], op=ALU.subtract), r=[t4], w=[cn])
                OP("dve", lambda e, t4=t4, cn=cn: e.tensor_tensor(out=cn[:, 656:672], in0=t4[:, 32:48], in1=t4[:, 48:64], op=ALU.add), r=[t4], w=[cn])
                banks = [pbank(), pbank(), pbank()]
                for bi in range(3):
                    def mmf(e, bi=bi, pb=banks[bi], xT=xT):
                        for j in range(4 * bi, min(4 * bi + 4, 11)):
                            for c in range(8):
                                i = e.matmul(pb[:, (j % 4) * 128:(j % 4 + 1) * 128], lhsT=win[:, c, F0 + j * 128:F0 + (j + 1) * 128], rhs=xT[:, c, :], start=(c == 0), stop=(c == 7))
                        if bi == 2:
                            for c in range(8):
                                i = e.matmul(pb[0:4, 384:512], lhsT=win[:, c, WOFF["ff"]:WOFF["ff"] + 4], rhs=xT[:, c, :], start=(c == 0), stop=(c == 7))
                        return i
                    OP("pe", mmf, r=[xT, win], w=[banks[bi]])
                FTf = FT.ap.rearrange("p c t -> p (c t)")
                OP("act", lambda e, FTf=FTf, pb=banks[0]: e.mul(out=FTf[:, 0:512], in_=pb.ap, mul=0.125), r=[banks[0]], w=[FT])
                OP("dve", lambda e, FTf=FTf, pb=banks[1]: e.tensor_scalar(out=FTf[:, 512:768], in0=pb[:, 0:256], scalar1=0.125, scalar2=None, op0=ALU.mult), r=[banks[1]], w=[FT])
                OP("dve", lambda e, FTf=FTf, pb=banks[1]: e.tensor_copy(out=FTf[:, 768:1024], in_=pb[:, 256:512]), r=[banks[1]], w=[FT])
                OP("act", lambda e, FTf=FTf, pb=banks[2]: e.copy(out=FTf[:, 1024:1408], in_=pb[:, 0:384]), r=[banks[2]], w=[FT])
                OP("act", lambda e, lfa=lfa, pb=banks[2]: e.activation(out=lfa.ap, in_=pb[0:4, 384:512], func=AF.Exp, scale=-1.0, bias=fbn.ap), r=[banks[2], fbn], w=[lfa])
                OP("act", lambda e, lfa=lfa: e.activation(out=lfa.ap, in_=lfa.ap, func=AF.Ln, bias=1.0), r=[lfa], w=[lfa])
                OP("dve", lambda e, lfa=lfa, lfo=lfo: e.tensor_scalar(out=lfo.ap, in0=lfa.ap, scalar1=-1.0, scalar2=None, op0=ALU.mult), r=[lfa], w=[lfo])
                DMA(QT[384:1152, rows].rearrange("(c p) t -> p c t", p=128), FT[:, 0:6, :], r=[FT])
                DMA(KT_in[288:800, rows].rearrange("(c p) t -> p c t", p=128), FT[:, 6:10, :], r=[FT])
                DMA(MR[:, rows], FT[:, 10, :], r=[FT])
                DMA(LF_in[:, rows], lfo.ap, r=[lfo])
                pv = pbank()
                OP("pe", proj_tok(pv, 512, WOFF["sbv"]), r=[xT, win], w=[pv])
                OP("dve", lambda e, vt=vt, pv=pv: e.tensor_copy(out=vt[:, 256:768], in_=pv.ap), r=[pv], w=[vt])
                for hf in range(2):
                    pg = pbank()
                    OP("pe", proj_tok(pg, 512, WOFF["gz"] + hf * 512), r=[xT, win], w=[pg])
                    OP("act", lambda e, gzt=gzt, pg=pg, hf=hf: e.activation(out=gzt[:, hf * 512:(hf + 1) * 512], in_=pg.ap, func=AF.Silu), r=[pg], w=[gzt])
                DMA(GZ[rows, :], gzt.ap, r=[gzt])

                yield
                pt = ptbank()

                def trc(e, pt=pt, cn=cn):
                    for c in range(5):
                        i = e.transpose(out=pt[:, c * 128:(c + 1) * 128], in_=cn[:, c * 128:(c + 1) * 128], identity=ident.ap)
                    i = e.transpose(out=pt[0:32, 640:768], in_=cn[:, 640:672], identity=ident.ap)
                    return i
                OP("pe", trc, r=[cn, ident], w=[pt])
                OP("act", lambda e, pt=pt, cT=cT: e.copy(out=cT.ap.rearrange("p c t -> p (c t)")[:, 0:640], in_=pt[:, 0:640]), r=[pt], w=[cT])
                OP("act", lambda e, pt=pt, cT=cT: e.copy(out=cT[0:32, 5, :], in_=pt[0:32, 640:768]), r=[pt], w=[cT])
                DMA(KT_in[256:288, rows], cT[0:32, 5, :], r=[cT])
                pq = pbank()

                def mmq(e, pq=pq, cT=cT):
                    for c in range(3):
                        i = e.matmul(pq[:, 0:384], lhsT=cT[:, c, :], rhs=wqb[:, c, :], start=(c == 0), stop=(c == 2))
                    return i
                OP("pe", mmq, r=[cT, wqb], w=[pq])
                pq3 = pq[:, 0:384].rearrange("p (h r) -> p h r", r=96)
                qf3 = qf.ap.rearrange("p (h r) -> p h r", r=96)
                OP("act", lambda e, pq3=pq3, qf3=qf3: e.mul(out=qf3[:, :, 0:64], in_=pq3[:, :, 0:64], mul=float(96.0 ** -0.5)), r=[pq], w=[qf])
                for h in range(4):
                    srcp = pq[:, h * 96 + 64:h * 96 + 96].rearrange("p (a i) -> p a i", a=2)
                    o = h * 64
                    OP("dve", lambda e, srcp=srcp, o=o: e.tensor_tensor(out=t8[:, o:o + 32].rearrange("p (a i) -> p a i", a=2), in0=srcp, in1=ropeq[:, tb, 0:32].rearrange("p (a i) -> p a i", a=2), op=ALU.mult), r=[pq, ropeq], w=[t8])
                    OP("dve", lambda e, srcp=srcp, o=o: e.tensor_tensor(out=t8[:, o + 32:o + 64].rearrange("p (a i) -> p a i", a=2), in0=srcp, in1=ropeq[:, tb, 16:48].rearrange("p (a i) -> p a i", a=2), op=ALU.mult), r=[pq, ropeq], w=[t8])
                    OP("dve", lambda e, o=o, h=h: e.tensor_tensor(out=qf[:, h * 96 + 64:h * 96 + 80], in0=t8[:, o:o + 16], in1=t8[:, o + 16:o + 32], op=ALU.subtract), r=[t8], w=[qf])
                    OP("dve", lambda e, o=o, h=h: e.tensor_tensor(out=qf[:, h * 96 + 80:h * 96 + 96], in0=t8[:, o + 32:o + 48], in1=t8[:, o + 48:o + 64], op=ALU.add), r=[t8], w=[qf])
                pkv = pbank()

                def mmkv(e, pkv=pkv, cT=cT):
                    for c in range(2):
                        i = e.matmul(pkv[:, 0:512], lhsT=cT[:, 3 + c, :], rhs=wkvb[:, c, :], start=(c == 0), stop=(c == 1))
                    return i
                OP("pe", mmkv, r=[cT, wkvb], w=[pkv])
                pkv3 = pkv.ap.rearrange("p (h r) -> p h r", r=128)
                OP("act", lambda e, pkv3=pkv3, vt=vt: e.copy(out=vt[:, 0:256].rearrange("p (h d) -> p h d", d=64), in_=pkv3[:, :, 64:128]), r=[pkv], w=[vt])
                OP("dve", lambda e, pkv3=pkv3, kn=kn: e.tensor_copy(out=kn.ap.rearrange("p (h d) -> p h d", d=64), in_=pkv3[:, :, 0:64]), r=[pkv], w=[kn])
                DMA(V_in[rows, :], vt.ap, r=[vt])
                yield
                pt = ptbank()

                def trq(e, pt=pt, qf=qf):
                    for h in range(4):
                        i = e.transpose(out=pt[0:96, h * 128:(h + 1) * 128], in_=qf[:, h * 96:(h + 1) * 96], identity=ident.ap)
                    return i
                OP("pe", trq, r=[qf, ident], w=[pt])
                OP("act", lambda e, pt=pt, qT=qT: e.copy(out=qT[0:96].rearrange("p h t -> p (h t)"), in_=pt[0:96, 0:512]), r=[pt], w=[qT])
                DMA(QT[0:384, rows].rearrange("(h r) t -> r h t", r=96), qT[0:96], r=[qT])
                pt = ptbank()

                def trk(e, pt=pt, kn=kn):
                    for c in range(2):
                        i = e.transpose(out=pt[:, c * 128:(c + 1) * 128], in_=kn[:, c * 128:(c + 1) * 128], identity=ident.ap)
                    return i
                OP("pe", trk, r=[kn, ident], w=[pt])
                OP("act", lambda e, pt=pt, knT=knT: e.copy(out=knT.ap.rearrange("p c t -> p (c t)"), in_=pt[:, 0:256]), r=[pt], w=[knT])
                DMA(KT_in[0:256, rows].rearrange("(c p) t -> p c t", p=128), knT.ap, r=[knT])

            gens = [gen_a(tb) for tb in range(NBO)]
            for tick in range(NBO + 3):
                for sk in range(4):
                    i = tick - sk
                    if 0 <= i < NBO:
                        next(gens[i], None)
            S_.barrier()

    KT_CH = [(0, 128), (128, 128), (256, 32), (288, 128), (416, 128), (544, 128), (672, 128)]
    VCH = min(1024, NOWN)

    def kt_grow(rk, R):
        for r0, n in KT_CH:
            if r0 <= R < r0 + n:
                return 2 * r0 + rk * n + (R - r0)
        raise AssertionError

    def load_cols_global(dst, src_g, row0, nrows, rows_per_rank, w):
        if NP == 1:
            DMA(dst, src_g[row0:row0 + nrows, :], w=w)
            return
        d4 = dst.rearrange("p (g m i) -> p g m i", m=8, i=128)
        for rk in range(2):
            gr = kt_grow(rk, row0) if rows_per_rank == 800 else rk * rows_per_rank + row0
            s5 = src_g[gr:gr + nrows, :].rearrange("p (g a b i) -> p g a b i", a=2, b=2, i=128)
            for a in range(2):
                for b in range(2):
                    DMA(d4[:, :, BLK2[rk][2 * a + b], :], s5[:, :, a, b, :], w=w)

    def load_v_global(dst, c0, w):
        if NP == 1:
            DMA(dst, V_g[:, c0:c0 + 64].rearrange("(n p) d -> p n d", p=128), w=w)
            return
        d4 = dst.rearrange("p (g m) d -> p g m d", m=8)
        GPC = VCH // 512
        for rk in range(2):
            for c in range(NOWN // VCH):
                base = c * 2 * VCH + rk * VCH
                s5 = V_g[base:base + VCH, c0:c0 + 64].rearrange("(g a b p) d -> p g a b d", a=2, b=2, p=128)
                for a in range(2):
                    for b in range(2):
                        DMA(d4[:, c * GPC:(c + 1) * GPC, BLK2[rk][2 * a + b], :], s5[:, :, a, b, :], w=w)

    def phase_fc(l):
        with ExitStack() as es:
            lfall = sb(es, "lfall", [4, S], F32)
            cz = sb(es, "cz", [4, S + 1], F32)
            CH = min(2048, S)
            onesf = sb(es, "onesf", [4, CH], F32)
            load_cols_global(lfall.ap, LF_g, 0, 4, 4, w=[lfall])
            OP("pool", lambda e: e.memset(onesf.ap, 1.0), w=[onesf])
            OP("pool", lambda e: e.memset(cz[:, 0:1], 0.0), w=[cz])
            for ch in range(S // CH):
                init = 0.0 if ch == 0 else cz[:, ch * CH:ch * CH + 1]
                OP("dve", lambda e, ch=ch, init=init: e.tensor_tensor_scan(out=cz[:, 1 + ch * CH:1 + (ch + 1) * CH], data0=onesf.ap, data1=lfall[:, ch * CH:(ch + 1) * CH], initial=init, op0=ALU.mult, op1=ALU.add), r=[onesf, lfall, cz], w=[cz])
            CW = 1024
            STK = [sb(es, "stk%d" % i, [4, 6, CW], BF16) for i in range(2)]
            R1 = [sb(es, "r1_%d" % i, [4, CW], F32) for i in range(2)]
            R2 = [sb(es, "r2_%d" % i, [4, CW], F32) for i in range(2)]
            for st in STK:
                OP("pool", lambda e, st=st: e.memset(st.ap, 1.0), w=[st])
            for ch in range(S // CW):
                st, r1, r2 = STK[ch % 2], R1[ch % 2], R2[ch % 2]
                c = cz[:, 1 + ch * CW:1 + (ch + 1) * CW]
                OP("dve", lambda e: e.tensor_scalar(out=st[:, 3, :], in0=c, scalar1=-1.0, scalar2=None, op0=ALU.mult), r=[cz], w=[st])
                OP("dve", lambda e: e.tensor_tensor(out=r1.ap, in0=c, in1=st[:, 3, :], op=ALU.add), r=[cz, st], w=[r1])
                OP("dve", lambda e: e.tensor_scalar(out=st[:, 4, :], in0=r1.ap, scalar1=-1.0, scalar2=None, op0=ALU.mult), r=[r1], w=[st])
                OP("dve", lambda e: e.tensor_tensor(out=r2.ap, in0=r1.ap, in1=st[:, 4, :], op=ALU.add), r=[r1, st], w=[r2])
                OP("dve", lambda e: e.tensor_scalar(out=st[:, 5, :], in0=r2.ap, scalar1=-1.0, scalar2=None, op0=ALU.mult), r=[r2], w=[st])
                DMA(CAK[:, :, ch * CW:(ch + 1) * CW], st.ap, r=[st])
            czb = cz[:, 0:S].rearrange("h (n i) -> h n i", i=128)[:, :, 0]
            refs = czb if NP == 1 else czb[:, 1::2]
            qh = sb(es, "qh", [4, NBO], BF16)
            ql = sb(es, "ql", [4, NBO], BF16)
            ql2 = sb(es, "ql2", [4, NBO], BF16)
            q1 = sb(es, "q1", [4, NBO], F32)
            q2 = sb(es, "q2", [4, NBO], F32)
            OP("dve", lambda e: e.tensor_copy(out=qh.ap, in_=refs), r=[cz], w=[qh])
            OP("dve", lambda e: e.tensor_tensor(out=q1.ap, in0=refs, in1=qh.ap, op=ALU.subtract), r=[cz, qh], w=[q1])
            OP("dve", lambda e: e.tensor_copy(out=ql.ap, in_=q1.ap), r=[q1], w=[ql])
            OP("dve", lambda e: e.tensor_tensor(out=q2.ap, in0=q1.ap, in1=ql.ap, op=ALU.subtract), r=[q1, ql], w=[q2])
            OP("dve", lambda e: e.tensor_copy(out=ql2.ap, in_=q2.ap), r=[q2], w=[ql2])
            NBC = min(8, NBO)
            STQ = [sb(es, "stq%d" % i, [4, 6, NBC * 128], BF16) for i in range(2)]
            for st in STQ:
                OP("pool", lambda e, st=st: e.memset(st.ap, 1.0), w=[st])
            for ch in range(NBO // NBC):
                st = STQ[ch % 2]
                for ri, src in enumerate((qh, ql, ql2)):
                    OP("dve", lambda e, ri=ri, src=src: e.tensor_copy(out=st[:, ri, :].rearrange("h (n i) -> h n i", i=128), in_=src[:, ch * NBC:(ch + 1) * NBC].unsqueeze(2).to_broadcast([4, NBC, 128])), r=[src], w=[st])
                DMA(CAQ[:, :, ch * NBC * 128:(ch + 1) * NBC * 128], st.ap, r=[st])
            S_.barrier()

    def qgroup_blocks(G):
        if NP == 1:
            return 4 * G, [(4 * G + d, d) for d in range(4)]
        return 8 * G, [(8 * G + d, d) for d in range(8)]

    def phase_b(l):
        with ExitStack() as es:
            KT_ = [sb(es, "KT%d" % i, [96, S], BF16) for i in range(2)]
            VV_ = [sb(es, "VV%d" % i, [128, NBLK, 65], BF16) for i in range(2)]
            QQ_ = [sb(es, "QQ%d" % i, [96, 512], BF16) for i in range(3)]
            QN_ = [sb(es, "QN%d" % i, [64, 512], BF16) for i in range(3)]
            PP_ = [sb(es, "PP%d" % i, [128, 512], BF16) for i in range(4)]
            EE_ = [sb(es, "EE%d" % i, [128, 512], F32) for i in range(3)]
            EX_ = [sb(es, "EX%d" % i, [128, 512], F32) for i in range(3)]
            SP_ = [sb(es, "SP%d" % i, [128, 512], BF16) for i in range(3)]
            OS_ = [sb(es, "OS%d" % i, [128, 4, 64], F32) for i in range(2)]
            RC_ = [sb(es, "RC%d" % i, [128, 4], F32) for i in range(2)]
            OA_ = [sb(es, "OA%d" % i, [128, 4, 64], F32) for i in range(2)]
            TMP_ = [sb(es, "TMPB%d" % i, [128, 4, 64], F32) for i in range(2)]
            RT_ = [sb(es, "RT%d" % i, [128, 4], F32) for i in range(2)]
            ER_ = [sb(es, "ER%d" % i, [128, 4], F32) for i in range(2)]
            for vv in VV_:
                OP("pool", lambda e, vv=vv: e.memset(vv.ap, 1.0), w=[vv])
            PO = [PB[0], PB[1]]
            PS = [PB[2], PB[3], PB[4], PB[5]]
            PVr = [T(PB[i][:, 0:256], "pvr%d" % i, psum=True, res=PB[i].res) for i in range(2)]
            TOTr = [T(PB[i][:, 256:260], "totr%d" % i, psum=True, res=PB[i].res) for i in range(2)]

            tasks = []
            for br in range(4):
                for h in range(4):
                    for G in range(NQG):
                        tasks.append({"br": br, "h": h, "G": G, "idx": len(tasks)})
            hb_of = {}

            def issue_head_loads(br, h):
                k = len(hb_of)
                hb_of[(br, h)] = k
                kt, vv = KT_[k % 2], VV_[k % 2]
                krow0 = (0, 288, 544)[br]
                load_cols_global(kt[0:64, :], KT_g, krow0 + h * 64, 64, 800, w=[kt])
                if br == 0:
                    load_cols_global(kt[64:96, :], KT_g, 256, 32, 800, w=[kt])
                if br == 2:
                    DMA(kt[64:70, :], CAK[h], w=[kt])
                load_v_global(vv[:, :, 0:64], br * 256 + h * 64, w=[vv])

            def issue_q_load(t):
                br, h, G = t["br"], t["h"], t["G"]
                qq = QQ_[t["idx"] % 3]
                t["qq"] = qq
                cols = slice(G * 512, (G + 1) * 512)
                if br == 3:
                    DMA(qq[0:64, :], QT[896 + h * 64:896 + (h + 1) * 64, cols], w=[qq])
                    return
                qrow0 = (0, 384, 640)[br]
                qstride = (96, 64, 64)[br]
                nq = 96 if br == 0 else 64
                DMA(qq[0:nq, :], QT[qrow0 + h * qstride:qrow0 + h * qstride + nq, cols], w=[qq])
                if br == 2:
                    DMA(qq[64:70, :], CAQ[h, :, cols], w=[qq])

            units = []
            for t in tasks:
                br, h, G = t["br"], t["h"], t["G"]
                if br == 3:
                    blocks = [(0, None), (1, None)]
                else:
                    nfull, diag = qgroup_blocks(G)
                    blocks = [(n, None) for n in range(nfull)] + diag
                    if br == 1:
                        blocks = blocks[::-1]
                for bi, (n, mid) in enumerate(blocks):
                    units.append({"t": t, "bi": bi, "n": n, "mid": mid, "last": bi == len(blocks) - 1, "u": len(units)})

            def first_of_task(u):
                return u["bi"] == 0

            def stage_prefetch(u):
                if not first_of_task(u):
                    return
                t = u["t"]
                ti = t["idx"]
                if ti == 0:
                    issue_head_loads(0, 0)
                    issue_q_load(tasks[0])
                if ti + 1 < len(tasks):
                    nt = tasks[ti + 1]
                    if nt["br"] < 3 and (nt["br"], nt["h"]) not in hb_of:
                        issue_head_loads(nt["br"], nt["h"])
                    issue_q_load(nt)

            def kv_of(t):
                k = hb_of[(t["br"], t["h"])]
                return KT_[k % 2], VV_[k % 2]

            def stage_s(u):
                stage_prefetch(u)
                t = u["t"]
                br, h, n, mid, qq = t["br"], t["h"], u["n"], u["mid"], t["qq"]
                if br == 3:
                    ps = PS[u["u"] % 4]
                    u["ps"] = ps
                    OP("pe", lambda e: e.matmul(ps.ap, lhsT=mkT[0:64, h, n * 128:(n + 1) * 128], rhs=qq[0:64, :], start=True, stop=True), r=[mkT, qq], w=[ps])
                elif br != 1:
                    kt, vv = kv_of(t)
                    dk = (96, 64, 70)[br]
                    ps = PS[u["u"] % 4]
                    u["ps"] = ps

                    if mid is None:
                        j0 = 0
                    elif NP == 1:
                        j0 = mid
                    else:
                        j0 = min(mid // 2, 3)
                    u["j0"] = j0
                    c0 = j0 * 128

                    def mms(e):
                        i = e.matmul(ps[:, c0:512], lhsT=kt[0:dk, n * 128:(n + 1) * 128], rhs=qq[0:dk, c0:512], start=True, stop=(mid is None))
                        if mid is not None:
                            i = e.matmul(ps[:, c0:512], lhsT=ident.ap, rhs=masks[:, mid, c0:512], start=False, stop=True)
                        return i
                    OP("pe", mms, r=[kt, qq, ident, masks], w=[ps])
                else:
                    kt, vv = kv_of(t)
                    pz = PS[u["u"] % 2]

                    def mmz(e):
                        i = e.matmul(pz.ap, lhsT=kt[0:64, n * 128:(n + 1) * 128], rhs=qq[0:64, :], start=True, stop=(mid is None))
                        if mid is not None:
                            i = e.matmul(pz.ap, lhsT=ident.ap, rhs=masks[:, NM + mid, :], start=False, stop=True)
                        return i
                    OP("pe", mmz, r=[kt, qq, ident, masks], w=[pz])
                    ee = EE_[u["u"] % 3]
                    u["ee"] = ee
                    spt = SP_[u["u"] % 3]
                    u["spt"] = spt
                    OP("act", lambda e: e.activation(out=ee.ap, in_=pz.ap, func=AF.Exp), r=[pz], w=[ee])
                    OP("act", lambda e: e.activation(out=spt.ap, in_=ee.ap, func=AF.Ln, bias=1.0), r=[ee], w=[spt])
                    return
                pp = PP_[u["u"] % 4]
                u["pp"] = pp
                c0 = u.get("j0", 0) * 128
                OP("act", lambda e: e.activation(out=pp[:, c0:512], in_=ps[:, c0:512], func=AF.Exp), r=[ps], w=[pp])

            def stage_l(u):
                t = u["t"]
                if t["br"] != 1:
                    return
                h, n, mid, spt = t["h"], u["n"], u["mid"], u["spt"]
                kt, vv = kv_of(t)
                pl = PS[2 + u["u"] % 2]
                totr = TOTr[u["u"] % 2]

                OP("pe", lambda e: e.matmul(pl.ap, lhsT=tri.ap, rhs=spt.ap, start=True, stop=True), r=[tri, spt], w=[pl])

                def mmt(e):
                    for j in range(4):
                        i = e.matmul(totr[:, j:j + 1], lhsT=spt[:, j * 128:(j + 1) * 128], rhs=onec.ap, start=True, stop=True, skip_group_check=True)
                    return i
                OP("pe", mmt, r=[spt, onec], w=[totr])
                aa = PP_[u["u"] % 4]
                u["pp"] = aa
                ex, ee = EX_[u["u"] % 3], u["ee"]
                OP("act", lambda e: e.activation(out=ex.ap, in_=pl.ap, func=AF.Exp, scale=-1.0), r=[pl], w=[ex])
                OP("pool", lambda e: e.tensor_tensor(out=aa.ap, in0=ee.ap, in1=ex.ap, op=ALU.mult), r=[ee, ex], w=[aa])

            def finalize_softmax(po, t):
                G = t["G"]
                col0 = t["br"] * 256 + t["h"] * 64
                rc, osb = RC_[t["idx"] % 2], OS_[t["idx"] % 2]
                po3 = po[:, 0:260].rearrange("p (j d) -> p j d", d=65)
                OP("dve", lambda e: e.reciprocal(out=rc.ap, in_=po3[:, :, 64]), r=[po], w=[rc])
                OP("dve", lambda e: e.tensor_tensor(out=osb.ap, in0=po3[:, :, 0:64], in1=rc.ap.unsqueeze(2).to_broadcast([128, 4, 64]), op=ALU.mult), r=[po, rc], w=[osb])
                DMA(Y[G * 512:(G + 1) * 512, col0:col0 + 64].rearrange("(j p) d -> p j d", p=128), osb.ap, r=[osb], q="pool")

            def stage_pv(u):
                t = u["t"]
                br, h, n, bi, pp = t["br"], t["h"], u["n"], u["bi"], u["pp"]
                if br != 1:
                    po = PO[t["idx"] % 2]
                    if br == 3:
                        rhs = mv[:, n, h, :]
                        rd = [pp, mv]
                    else:
                        kt, vv = kv_of(t)
                        rhs = vv[:, n, :]
                        rd = [pp, vv]

                    def mmo(e):
                        for j in range(u.get("j0", 0), 4):
                            i = e.matmul(po[:, j * 65:(j + 1) * 65], lhsT=pp[:, j * 128:(j + 1) * 128], rhs=rhs, start=(bi == 0 and j == 0), stop=u["last"], skip_group_check=True)
                        return i
                    OP("pe", mmo, r=rd, w=[po])
                    if u["last"]:
                        finalize_softmax(po, t)
                    return
                kt, vv = kv_of(t)
                pvr, totr = PVr[u["u"] % 2], TOTr[u["u"] % 2]
                oa, TMP, RT, ER = OA_[t["idx"] % 2], TMP_[t["idx"] % 2], RT_[t["idx"] % 2], ER_[t["idx"] % 2]

                def mmpv(e):
                    for j in range(4):
                        i = e.matmul(pvr[:, j * 64:(j + 1) * 64], lhsT=pp[:, j * 128:(j + 1) * 128], rhs=vv[:, n, 0:64], start=True, stop=True, skip_group_check=True)
                    return i
                OP("pe", mmpv, r=[pp, vv], w=[pvr])
                oaf = oa.ap.rearrange("p j d -> p (j d)")
                if bi == 0:
                    OP("dve", lambda e: e.tensor_copy(out=oaf, in_=pvr.ap), r=[pvr], w=[oa])
                    OP("dve", lambda e: e.tensor_copy(out=RT.ap, in_=totr.ap), r=[totr], w=[RT])
                else:
                    OP("act", lambda e: e.activation(out=ER.ap, in_=RT.ap, func=AF.Exp, scale=-1.0), r=[RT], w=[ER])
                    OP("dve", lambda e: e.tensor_tensor(out=TMP.ap, in0=pvr.ap.rearrange("p (j d) -> p j d", d=64), in1=ER.ap.unsqueeze(2).to_broadcast([128, 4, 64]), op=ALU.mult), r=[pvr, ER], w=[TMP])
                    OP("dve", lambda e: e.tensor_tensor(out=oa.ap, in0=oa.ap, in1=TMP.ap, op=ALU.add), r=[oa, TMP], w=[oa])
                    OP("dve", lambda e: e.tensor_tensor(out=RT.ap, in0=RT.ap, in1=totr.ap, op=ALU.add), r=[RT, totr], w=[RT])
                if u["last"]:
                    G = t["G"]
                    col0 = 256 + h * 64
                    DMA(Y[G * 512:(G + 1) * 512, col0:col0 + 64].rearrange("(j p) d -> p j d", p=128), oa.ap, r=[oa], q="pool")

            stages = [(stage_s, 0), (stage_l, 1), (stage_pv, 2)]
            nU = len(units)
            for tick in range(nU + 2):
                for f, sk in stages:
                    i = tick - sk
                    if 0 <= i < nU:
                        f(units[i])
            S_.barrier()

    def phase_c(l, last):
        with ExitStack() as es:
            wbr = sb(es, "wbr", [128, 8, D], BF16)
            DMA(wbr.ap, wbr_d[l].rearrange("n (c p) d -> p (n c) d", p=128), w=[wbr], q="pool")
            wmu = sb(es, "wmu", [128, 4096], BF16)
            DMA(wmu.ap, wmu_d[l], w=[wmu], q="pool")
            wo = sb(es, "wo", [128, 8, D], BF16)
            DMA(wo.ap, wout_d[l].rearrange("(c p) d -> p c d", p=128), w=[wo], q="pool")
            lng = sb(es, "lng", [128, D], F32)
            DMA(lng.ap, lng_d[l].partition_broadcast(128), w=[lng])
            lnb = sb(es, "lnb", [128, D], F32)
            DMA(lnb.ap, lnb_d[l].partition_broadcast(128), w=[lnb])

            def dbl(name, shape, dtype):
                return [sb(es, name + str(i), shape, dtype) for i in range(3)]
            def quad(name, shape, dtype):
                return [sb(es, name + str(i), shape, dtype) for i in range(4)]
            YT_ = quad("yt", [128, D], F32)
            GT_ = quad("gt", [128, D], F32)
            XT_ = quad("xc", [128, D], F32)
            MRT_ = quad("mrT", [128, 128], BF16)
            YB_ = dbl("yb", [128, D], BF16)
            YBT_ = dbl("ybT", [128, 8, 128], BF16)
            MGR = [sb(es, "mgr%d" % i, [128, 512], F32) for i in range(4)]
            MER0_ = dbl("mer0_", [128, 512], F32)
            MER1_ = dbl("mer1_", [128, 512], F32)
            nhalf = sb(es, "nhalf", [128, 1], F32)
            OP("pool", lambda e: e.memset(nhalf.ap, -0.5), w=[nhalf])
            mgc = [0]
            MB_ = dbl("mb", [128, D], BF16)
            MT_ = dbl("mT", [128, 8, 128], BF16)
            Z_ = dbl("z", [128, D], F32)
            ST_ = dbl("st", [128, 2, 6], F32)
            MV_ = dbl("mvar", [128, 2], F32)
            RS_ = dbl("rsd", [128, 1], F32)
            src = x_in if l == 0 else XR
            dst = y_out if last else XR
            def gen_c(tb):
                k2 = tb % 3
                k4 = tb % 4
                yt, gt, xt, mrT, yb, ybT, mb, mT, z, st, mvar, rsd = (
                    YT_[k4], GT_[k4], XT_[k4], MRT_[k4], YB_[k2], YBT_[k2], MB_[k2], MT_[k2],
                    Z_[k2], ST_[k2], MV_[k2], RS_[k2])
                merH = (MER0_[k2], MER1_[k2])
                rows = slice(tb * 128, (tb + 1) * 128)
                DMA(yt.ap, Y[rows, :], w=[yt])
                DMA(gt.ap, GZ[rows, :], w=[gt])
                DMA(xt.ap, src[rows, :], w=[xt])
                DMA(mrT.ap, MR[:, rows], w=[mrT])
                yield
                OP("dve", lambda e: e.tensor_tensor(out=yb.ap, in0=yt.ap, in1=gt.ap, op=ALU.mult), r=[yt, gt], w=[yb])
                pt = ptbank()

                def tr8(src_t, pt):
                    def f(e):
                        for c in range(8):
                            i = e.transpose(out=pt[:, c * 128:(c + 1) * 128], in_=src_t[:, c * 128:(c + 1) * 128], identity=ident.ap)
                        return i
                    return f
                OP("pe", tr8(yb, pt), r=[yb, ident], w=[pt])
                OP("act", lambda e, pt=pt: e.copy(out=ybT.ap.rearrange("p c t -> p (c t)"), in_=pt.ap), r=[pt], w=[ybT])
                for n in range(4):
                    for hf in range(2):
                        pbr = pbank()
                        pg = pbank()
                        hs = slice(hf * 512, (hf + 1) * 512)

                        def mmb(e, n=n, hs=hs, pbr=pbr):
                            for c in range(2):
                                i = e.matmul(pbr.ap, lhsT=ybT[:, 2 * n + c, :], rhs=wbr[:, 2 * n + c, hs], start=(c == 0), stop=(c == 1))
                            return i
                        OP("pe", mmb, r=[ybT, wbr], w=[pbr])
                        OP("pe", lambda e, n=n, hf=hf, pg=pg: e.matmul(pg.ap, lhsT=mrT.ap, rhs=wmu[:, n * 1024 + hf * 512:n * 1024 + (hf + 1) * 512], start=True, stop=True), r=[mrT, wmu], w=[pg])
                        mgc[0] += 1
                        mgb = MGR[mgc[0] % 4]
                        mh = merH[hf]
                        OP("act", lambda e, pg=pg, mgb=mgb: e.activation(out=mgb.ap, in_=pg.ap, func=AF.Sigmoid), r=[pg], w=[mgb])
                        if n == 0:
                            OP("dve", lambda e, pbr=pbr, mgb=mgb, mh=mh: e.tensor_tensor(out=mh.ap, in0=mgb.ap, in1=pbr.ap, op=ALU.mult), r=[mgb, pbr], w=[mh])
                        else:
                            OP("dve", lambda e, pbr=pbr, mgb=mgb: e.tensor_tensor(out=mgb.ap, in0=mgb.ap, in1=pbr.ap, op=ALU.mult), r=[mgb, pbr], w=[mgb])
                            OP("pool", lambda e, mgb=mgb, mh=mh: e.tensor_tensor(out=mh.ap, in0=mh.ap, in1=mgb.ap, op=ALU.add), r=[mh, mgb], w=[mh])
                for hf in range(2):
                    OP("act", lambda e, hf=hf: e.copy(out=mb[:, hf * 512:(hf + 1) * 512], in_=merH[hf].ap), r=[merH[hf]], w=[mb])
                yield
                pt = ptbank()
                OP("pe", tr8(mb, pt), r=[mb, ident], w=[pt])
                OP("act", lambda e, pt=pt: e.copy(out=mT.ap.rearrange("p c t -> p (c t)"), in_=pt.ap), r=[pt], w=[mT])
                for hf in range(2):
                    po = pbank()
                    hs = slice(hf * 512, (hf + 1) * 512)

                    def mmo(e, hs=hs, po=po):
                        for c in range(8):
                            i = e.matmul(po.ap, lhsT=mT[:, c, :], rhs=wo[:, c, hs], start=(c == 0), stop=(c == 7))
                        return i
                    OP("pe", mmo, r=[mT, wo], w=[po])
                    OP("dve", lambda e, hs=hs, po=po: e.scalar_tensor_tensor(out=z[:, hs], in0=xt[:, hs], scalar=float(depth_alpha), in1=po.ap, op0=ALU.mult, op1=ALU.add), r=[xt, po], w=[z])
                    OP("dve", lambda e, hs=hs, hf=hf: e.bn_stats(out=st[:, hf, :], in_=z[:, hs]), r=[z], w=[st])
                OP("dve", lambda e: e.bn_aggr(out=mvar.ap, in_=st.ap.rearrange("p a s -> p (a s)")), r=[st], w=[mvar])
                OP("dve", lambda e: e.tensor_scalar(out=rsd.ap, in0=mvar[:, 1:2], scalar1=LN_EPS, scalar2=None, op0=ALU.add), r=[mvar], w=[rsd])
                OP("pool", lambda e: e.tensor_tensor(out=rsd.ap, in0=rsd.ap, in1=nhalf.ap, op=ALU.pow), r=[rsd, nhalf], w=[rsd])
                yield
                OP("dve", lambda e: e.tensor_scalar(out=z.ap, in0=z.ap, scalar1=mvar[:, 0:1], scalar2=rsd.ap, op0=ALU.subtract, op1=ALU.mult), r=[z, mvar, rsd], w=[z])
                OP("dve", lambda e: e.tensor_tensor(out=z.ap, in0=z.ap, in1=lng.ap, op=ALU.mult), r=[z, lng], w=[z])
                OP("pool", lambda e: e.tensor_tensor(out=z.ap, in0=z.ap, in1=lnb.ap, op=ALU.add), r=[z, lnb], w=[z])
                t = DMA(dst[rows, :], z.ap, r=[z])
                if last:
                    S_.out_tickets.append(t)


            gens = [gen_c(tb) for tb in range(NBO)]
            for tick in range(NBO + 3):
                for sk in range(4):
                    i = tick - sk
                    if 0 <= i < NBO:
                        next(gens[i], None)
            S_.barrier()

    def gather(l):
        if NP == 1:
            return
        groups = [[2 * i, 2 * i + 1] for i in range(4)]
        pairs = [(KT_in[r0:r0 + n, :], KT_g[2 * r0:2 * r0 + 2 * n, :]) for r0, n in KT_CH]
        pairs += [(V_in[c * VCH:(c + 1) * VCH, :], V_g[2 * c * VCH:2 * (c + 1) * VCH, :]) for c in range(NOWN // VCH)]
        pairs += [(LF_in, LF_g)]
        for src_t, dst_t in pairs:
            ccs["n"] += 1
            nc.gpsimd.collective_compute("AllGather", ALU.bypass, replica_groups=groups, ins=[src_t.opt()], outs=[dst_t.opt()]).then_inc(ccs["sem"], 1)
        for e in ("pe", "act", "dve", "pool", "sp"):
            S_.eng[e].wait_ge(ccs["sem"], ccs["n"])
        S_.barrier()

    ccs = {"n": 0, "sem": nc.alloc_semaphore("cc_sem") if NP > 1 else None}
    import os
    stop = os.environ.get("KSTOP", "")
    for l in range(L):
        if stop == "setup":
            break
        phase_a(l)
        if stop == "a":
            break
        gather(l)
        phase_fc(l)
        if stop == "fc":
            break
        phase_b(l)
        if stop == "b":
            break
        phase_c(l, l == L - 1)
    for t in S_.out_tickets:
        S_._wait("sp", t)
    ges.close()
    return nc


def make_masks(NP, rank):
    NM = 4 if NP == 1 else 8
    qblk = (0, 1, 2, 3) if NP == 1 else BLK2[rank]
    m = np.zeros((128, 2 * NM, 512), np.float32)
    k = np.arange(128)[:, None]
    i = np.arange(128)[None, :]
    for strict in range(2):
        for d in range(NM):
            for j, qb in enumerate(qblk):
                if d < qb:
                    blk = np.zeros((128, 128), np.float32)
                elif d > qb:
                    blk = np.full((128, 128), NEG, np.float32)
                else:
                    ok = (k < i) if strict else (k <= i)
                    blk = np.where(ok, 0.0, NEG).astype(np.float32)
                m[:, strict * NM + d, j * 128:(j + 1) * 128] = blk
    return m


def own_blocks(S, NP, rank):
    if NP == 1:
        return list(range(S // 128))
    return [8 * g + m for g in range(S // 1024) for m in BLK2[rank]]


_PROG_CACHE = {}


def run(inputs, S, L, NP):
    key = (S, L, NP)
    alpha = (2 * L) ** 0.25
    if key not in _PROG_CACHE:
        _PROG_CACHE[key] = build_program(S, L, NP, alpha)
    nc = _PROG_CACHE[key]
    x = np.asarray(inputs["x"], np.float32)
    B = x.shape[0]
    in_maps = []
    meta = []
    wnames = ["w_in", "mla_q_norm", "mla_w_qb", "mla_kv_norm", "mla_w_kvb", "fox_forget_bias", "w_mem_kv", "w_merge_up",
              "w_branch", "w_out", "ln_gain", "ln_bias"]
    w = {k: np.ascontiguousarray(np.asarray(inputs[k], np.float32)) for k in wnames}
    pos = np.asarray(inputs["positions"]).astype(np.int32)
    for core in range(8):
        if NP == 1:
            b, rank = core % B, 0
        else:
            b, rank = core // 2, core % 2
        blks = own_blocks(S, NP, rank)
        tok = np.concatenate([np.arange(n * 128, (n + 1) * 128) for n in blks])
        m = dict(w)
        m["x"] = np.ascontiguousarray(x[b][tok])
        m["mem"] = np.ascontiguousarray(np.asarray(inputs["mem"], np.float32)[b])
        m["pos"] = np.ascontiguousarray(pos[b][tok].reshape(len(blks), 128).T)
        m["masks"] = make_masks(NP, rank)
        in_maps.append(m)
        meta.append((b, tok))
    res = run_bass_kernel_spmd(nc, in_maps, core_ids=list(range(8)))
    out = np.zeros_like(x)
    ncores = 8 if NP == 2 else B
    for core in range(ncores):
        b, tok = meta[core]
        out[b][tok] = res.results[core]["y"]
    return out


def kernel(**inputs):
    return run(inputs, 8192, 4, 2)
```
